# Optimizing a Trainium2 kernel written in Bass

```python
import math
import jax, jax.numpy as jnp
from jax import lax
import numpy as np

D_MODEL = 1024
BATCH = 8
SEQ = 2048
DEPTH = 1
DEC_BATCH = 8
DEC_SEQ = 64
PAST_LEN = 2048

CHUNK = 64
A_HEADS = 4
A_DK = 128
A_DV = 128
A_WIDTH = A_HEADS * A_DV
B_HEADS = 8
B_KV_HEADS = 4
B_HD = 64
B_WIDTH = B_HEADS * B_HD
IDX_HEADS = 8
IDX_DIM = 64
TOPK_MAX = 256
QBLOCK = 128
ROT_FRAC = 4
ROPE_THETA = 500000.0
MIX_WIDTH = A_WIDTH + B_WIDTH
D_FF = 4 * D_MODEL
EPS = 1e-6
IN_SIZES = (A_HEADS * A_DK, A_HEADS * A_DK, A_WIDTH, A_WIDTH,
            B_HEADS * B_HD, B_KV_HEADS * B_HD, B_KV_HEADS * B_HD,
            IDX_HEADS * IDX_DIM, IDX_DIM, IDX_HEADS)
IN_WIDTH = sum(IN_SIZES)

kernel_name = "hymba_hgrn2_dsa_streaming_step"


def rmsnorm(x, g):
    xf = x.astype(jnp.float32)
    y = xf * lax.rsqrt(jnp.mean(jnp.square(xf), axis=-1, keepdims=True) + EPS)
    return (y * g.astype(jnp.float32)).astype(x.dtype)


def split_cols(z):
    pts, acc = [], 0
    for s in IN_SIZES[:-1]:
        acc += s
        pts.append(acc)
    return jnp.split(z, pts, axis=-1)


def partial_rope(x, pos):
    rot = x.shape[-1] // ROT_FRAC
    half = rot // 2
    inv = jnp.power(ROPE_THETA, -jnp.arange(half, dtype=jnp.float32) * (2.0 / rot))
    ang = pos.astype(jnp.float32)[:, None] * inv[None, :]
    cos = jnp.cos(ang)[None, :, None, :]
    sin = jnp.sin(ang)[None, :, None, :]
    xr = x[..., :rot].astype(jnp.float32)
    x1, x2 = xr[..., :half], xr[..., half:]
    out = jnp.concatenate([x1 * cos - x2 * sin, x2 * cos + x1 * sin], axis=-1).astype(x.dtype)
    return jnp.concatenate([out, x[..., rot:]], axis=-1)


def hgrn2_mix(q_pre, f_pre, v, gate_pre, lb, s0, g_norm):
    bsz, T = q_pre.shape[:2]
    C = min(CHUNK, T)
    N = T // C
    q = jax.nn.silu(q_pre.astype(jnp.float32))
    f = lb + (1.0 - lb) * jax.nn.sigmoid(f_pre.astype(jnp.float32))
    logf = jnp.log(f)
    k = 1.0 - f
    vf = v.astype(jnp.float32)

    def to_chunks(a):
        return jnp.moveaxis(a.reshape(bsz, N, C, *a.shape[2:]), 1, 0)

    mask = jnp.tril(jnp.ones((C, C), dtype=bool))[None, :, :, None, None]

    def step(S, inp):
        qc, kc, vc, lc = inp
        b = jnp.cumsum(lc, axis=1)
        diff = b[:, :, None] - b[:, None, :]
        decay = jnp.exp(jnp.where(mask, diff, -jnp.inf))
        attn = jnp.einsum('bthk,bshk,btshk->bhts', qc, kc, decay)
        o = (jnp.einsum('bhts,bshv->bthv', attn, vc)
             + jnp.einsum('bthk,bhkv->bthv', qc * jnp.exp(b), S))
        bl = b[:, -1]
        kd = kc * jnp.exp(bl[:, None] - b)
        S_new = jnp.exp(bl)[..., None] * S + jnp.einsum('bshk,bshv->bhkv', kd, vc)
        return S_new, o

    S, o = lax.scan(step, s0.astype(jnp.float32),
                    (to_chunks(q), to_chunks(k), to_chunks(vf), to_chunks(logf)))
    o = jnp.moveaxis(o, 0, 1).reshape(bsz, T, A_HEADS, A_DV)
    o = rmsnorm(o, g_norm.reshape(A_HEADS, A_DV)) * jax.nn.silu(gate_pre.astype(jnp.float32))
    return o.reshape(bsz, T, A_WIDTH), S


def dsa_attend(q, k_all, v_all, qi, ki_all, wi, q_pos):
    bsz, Tq = q.shape[:2]
    Tk = k_all.shape[1]
    topk = min(TOPK_MAX, Tk // 4)
    qb = min(QBLOCK, Tq)
    nb = Tq // qb
    G = B_HEADS // B_KV_HEADS
    scale = B_HD ** -0.5
    key_chunk = jnp.arange(Tk) // CHUNK
    ki32 = ki_all.astype(jnp.float32)

    def blocks(a):
        return jnp.moveaxis(a.reshape(bsz, nb, qb, *a.shape[2:]), 1, 0)

    def block(args):
        qc, qic, wic, qp = args
        qchunk = qp // CHUNK
        s_idx = jnp.einsum('bqhd,bkd->bqhk', qic.astype(jnp.float32), ki32)
        score = jnp.einsum('bqh,bqhk->bqk', wic.astype(jnp.float32), jax.nn.relu(s_idx))
        visible = key_chunk[None, :] <= qchunk[:, None]
        score = jnp.where(visible[None], score, -jnp.inf)
        _, sel = lax.top_k(score, topk)
        valid = (sel // CHUNK) <= qchunk[None, :, None]
        kg = jax.vmap(lambda kk, ss: kk[ss])(k_all, sel)
        vg = jax.vmap(lambda vv, ss: vv[ss])(v_all, sel)
        qg = qc.reshape(bsz, qb, B_KV_HEADS, G, B_HD).astype(jnp.float32)
        logits = jnp.einsum('bqngd,bqjnd->bqngj', qg, kg.astype(jnp.float32)) * scale
        logits = jnp.where(valid[:, :, None, None, :], logits, -jnp.inf)
        p = jax.nn.softmax(logits, axis=-1)
        o = jnp.einsum('bqngj,bqjnd->bqngd', p, vg.astype(jnp.float32))
        return o.reshape(bsz, qb, B_WIDTH)

    o = lax.map(block, (blocks(q), blocks(qi), blocks(wi), q_pos.reshape(nb, qb)))
    return jnp.moveaxis(o, 0, 1).reshape(bsz, Tq, B_WIDTH)


def trunk_layer(x, c, pos, s0, k_past, v_past, ki_past,
                w_mod, b_mod, norm1, w_in, lb, g_norm_a, w_out, norm2, w_ff1, w_ff2):
    bsz, T = x.shape[:2]
    mod = (jax.nn.silu(c) @ w_mod + b_mod)[:, None, :]
    sh1, sc1, g1, sh2, sc2, g2 = jnp.split(mod, 6, axis=-1)
    h = rmsnorm(x, norm1) * (1.0 + sc1) + sh1
    z = h @ w_in
    qa, fa, ia, ga, qB, kB, vB, qI, kI, wI = split_cols(z)
    o_a, S = hgrn2_mix(qa.reshape(bsz, T, A_HEADS, A_DK), fa.reshape(bsz, T, A_HEADS, A_DK),
                       ia.reshape(bsz, T, A_HEADS, A_DV), ga.reshape(bsz, T, A_HEADS, A_DV),
                       lb, s0, g_norm_a)
    qB = partial_rope(qB.reshape(bsz, T, B_HEADS, B_HD), pos)
    kB = partial_rope(kB.reshape(bsz, T, B_KV_HEADS, B_HD), pos)
    vB = vB.reshape(bsz, T, B_KV_HEADS, B_HD)
    qI = partial_rope(qI.reshape(bsz, T, IDX_HEADS, IDX_DIM), pos)
    kI = partial_rope(kI[:, :, None, :], pos)[:, :, 0, :]
    wI = wI * ((IDX_HEADS * IDX_DIM) ** -0.5)
    if k_past is None:
        k_all, v_all, ki_all = kB, vB, kI
    else:
        k_all = jnp.concatenate([k_past.astype(kB.dtype), kB], axis=1)
        v_all = jnp.concatenate([v_past.astype(vB.dtype), vB], axis=1)
        ki_all = jnp.concatenate([ki_past.astype(kI.dtype), kI], axis=1)
    o_b = dsa_attend(qB, k_all, v_all, qI, ki_all, wI, pos)
    mix = jnp.concatenate([o_a.astype(x.dtype), o_b.astype(x.dtype)], axis=-1) @ w_out
    x = x + g1 * mix
    h2 = rmsnorm(x, norm2) * (1.0 + sc2) + sh2
    x = x + g2 * (jnp.square(jax.nn.relu(h2 @ w_ff1)) @ w_ff2)
    return x, kB, vB, kI, S


def setup_inputs(seed: int = 0) -> dict:
    key = jax.random.key(seed)
    ks = jax.random.split(key, 20)
    f32 = jnp.float32

    def nrm(k, shape, s):
        return jax.random.normal(k, shape, f32) * s

    return {
        "x_prompt": nrm(ks[0], (BATCH, SEQ, D_MODEL), 1.0),
        "x_sample": nrm(ks[1], (DEC_BATCH, DEC_SEQ, D_MODEL), 1.0),
        "cache_k": nrm(ks[2], (DEPTH, DEC_BATCH, PAST_LEN, B_KV_HEADS, B_HD), 1.0),
        "cache_v": nrm(ks[3], (DEPTH, DEC_BATCH, PAST_LEN, B_KV_HEADS, B_HD), 1.0),
        "cache_k_idx": nrm(ks[4], (DEPTH, DEC_BATCH, PAST_LEN, IDX_DIM), 1.0),
        "state_hgrn": nrm(ks[5], (DEPTH, DEC_BATCH, A_HEADS, A_DK, A_DV), 0.5),
        "c_prompt": nrm(ks[6], (BATCH, D_MODEL), 1.0),
        "c_sample": nrm(ks[7], (DEC_BATCH, D_MODEL), 1.0),
        "w_mod": nrm(ks[8], (DEPTH, D_MODEL, 6 * D_MODEL), 0.2 * D_MODEL ** -0.5),
        "b_mod": nrm(ks[9], (DEPTH, 6 * D_MODEL), 0.02),
        "norm1": 1.0 + nrm(ks[10], (DEPTH, D_MODEL), 0.02),
        "w_in": nrm(ks[11], (DEPTH, D_MODEL, IN_WIDTH), D_MODEL ** -0.5),
        "lb_logits": nrm(ks[12], (DEPTH + 1, A_HEADS * A_DK), 0.5),
        "g_norm_a": 1.0 + nrm(ks[13], (DEPTH, A_WIDTH), 0.02),
        "w_out": nrm(ks[14], (DEPTH, MIX_WIDTH, D_MODEL), MIX_WIDTH ** -0.5),
        "norm2": 1.0 + nrm(ks[15], (DEPTH, D_MODEL), 0.02),
        "w_ff1": nrm(ks[16], (DEPTH, D_MODEL, D_FF), D_MODEL ** -0.5),
        "w_ff2": nrm(ks[17], (DEPTH, D_FF, D_MODEL), D_FF ** -0.5),
        "norm_f": 1.0 + nrm(ks[18], (D_MODEL,), 0.02),
    }


def reference(x_prompt, x_sample, cache_k, cache_v, cache_k_idx, state_hgrn, c_prompt, c_sample,
              w_mod, b_mod, norm1, w_in, lb_logits, g_norm_a, w_out, norm2, w_ff1, w_ff2, norm_f):
    lb_all = jnp.cumsum(jax.nn.softmax(lb_logits.astype(jnp.float32), axis=0), axis=0)
    past = cache_k.shape[2]
    pos_p = jnp.arange(x_prompt.shape[1])
    pos_s = past + jnp.arange(x_sample.shape[1])
    hp, hs = x_prompt, x_sample
    kp_l, vp_l, kip_l, sp_l = [], [], [], []
    ks_l, vs_l, kis_l, ss_l = [], [], [], []
    for l in range(DEPTH):
        lw = (w_mod[l], b_mod[l], norm1[l], w_in[l], lb_all[l].reshape(A_HEADS, A_DK),
              g_norm_a[l], w_out[l], norm2[l], w_ff1[l], w_ff2[l])
        s0 = jnp.zeros((hp.shape[0], A_HEADS, A_DK, A_DV), jnp.float32)
        hp, kp, vp, kip, sp = trunk_layer(hp, c_prompt, pos_p, s0, None, None, None, *lw)
        hs, k_s, v_s, ki_s, s_s = trunk_layer(hs, c_sample, pos_s, state_hgrn[l],
                                              cache_k[l], cache_v[l], cache_k_idx[l], *lw)
        kp_l.append(kp); vp_l.append(vp); kip_l.append(kip); sp_l.append(sp)
        ks_l.append(k_s); vs_l.append(v_s); kis_l.append(ki_s); ss_l.append(s_s)
    y_prompt = rmsnorm(hp, norm_f)
    y_sample = rmsnorm(hs, norm_f)
    return (y_prompt, y_sample,
            jnp.stack(kp_l), jnp.stack(vp_l), jnp.stack(kip_l), jnp.stack(sp_l),
            jnp.stack(ks_l), jnp.stack(vs_l), jnp.stack(kis_l), jnp.stack(ss_l))
```

```python
from contextlib import ExitStack
import numpy as np
import concourse.bass as bass
import concourse.mybir as mybir
from concourse.bass_utils import run_bass_kernel_spmd

F32 = mybir.dt.float32
BF16 = mybir.dt.bfloat16
ALU = mybir.AluOpType
AF = mybir.ActivationFunctionType
AX = mybir.AxisListType

ENG = ["pe", "act", "dve", "pool", "sp"]
NPOOL = 12
DEBUG = False
CFG = {"A": True, "B": True, "ntiles": 16, "sample": True, "hgrn": True, "dsa": True, "wout": True, "nblk": 8, "scp": True}

D = 1024
TP = 2048
TS = 64
PAST = 2048
NTOK = TP + TS
NKEYMAX = 2176
NZ = 3720
Z_QA, Z_FA, Z_IA, Z_GA = 0, 512, 1024, 1536
Z_QB, Z_KB, Z_QI, Z_KI, Z_KI2, Z_VB, Z_WI = 2048, 2560, 2816, 3328, 3392, 3456, 3712
QB_PERM = [0, 2, 1, 3, 4, 6, 5, 7]
NROPE = 22
NIT = 17
EPS = 1e-6
NEG = -60000.0
ZBLOCKS = [(i * 512, min(512, NZ - i * 512)) for i in range(8)]


def w_in_segments():
    segs = [(0, 0, 512), (512, 512, 512), (1024, 1024, 512), (1536, 1536, 512)]
    for j, h in enumerate(QB_PERM):
        segs.append((Z_QB + 64 * j, 2048 + 64 * h, 64))
    segs.append((Z_KB, 2560, 256))
    segs.append((Z_QI, 3072, 512))
    segs.append((Z_KI, 3584, 64))
    segs.append((Z_KI2, 3584, 64))
    segs.append((Z_VB, 2816, 256))
    segs.append((Z_WI, 3648, 8))
    return segs


class Res:
    __slots__ = ("name", "w", "r", "excl")

    def __init__(self, name="", excl=False):
        self.name = name
        self.w = None
        self.r = {}
        self.excl = excl


class Sched:
    def __init__(self, nc, stack):
        self.nc = nc
        self.ops = {e: [] for e in ENG}
        self.tick = {e: 0 for e in ENG}
        self.sem = {e: stack.enter_context(nc.semaphore("s_" + e)) for e in ENG}
        self.known = {e: {} for e in ENG}
        self.dq = ("sp", "pool", "act")
        self.dsem = {q: [stack.enter_context(nc.semaphore("d_%s%d" % (q, i))) for i in range(NPOOL)]
                     for q in self.dq}
        self.dcnt = {q: 0 for q in self.dq}
        self.duse = {q: [0] * NPOOL for q in self.dq}
        self.all_dma_ev = {}
        self.same_eng = {"pool", "dve", "act"}

    def _semof(self, k):
        if k[0] == "e":
            return self.sem[k[1]]
        return self.dsem[k[1]][k[2]]

    def _collect(self, eng, reads, writes, eng_is_dma=False):
        waits = {}
        known = self.known[eng]

        def need(ev, war=False):
            if ev is None:
                return
            k, v = ev
            if k == ("e", eng):
                if eng not in self.same_eng:
                    return
            if known.get(k, 0) >= v:
                return
            if waits.get(k, 0) < v:
                waits[k] = v

        is_dma = eng_is_dma
        for r in reads:
            if r.w:
                for k, v in r.w.items():
                    need((k, v))
        for w in writes:
            if w.w:
                for k, v in w.w.items():
                    if is_dma and k[0] == "d":
                        continue
                    need((k, v))
            for k, v in w.r.items():
                need((k, v), war=True)
        return waits

    def _commit(self, eng, waits, ev, reads, writes):
        for k, v in waits.items():
            self.known[eng][k] = v
        k, v = ev
        for r in reads:
            if r.r.get(k, 0) < v:
                r.r[k] = v
        for w in writes:
            if ev[0][0] == "d" and w.w and all(kk[0] == "d" for kk in w.w):
                w.w[ev[0]] = ev[1]
            else:
                w.w = {ev[0]: ev[1]}
            w.r = {}

    def op(self, eng, fn, reads=(), writes=()):
        ex = [r for r in reads if r.excl]
        if ex:
            writes = list(writes) + [r for r in ex if r not in writes]
            reads = [r for r in reads if not r.excl]
        waits = self._collect(eng, reads, writes)
        self.tick[eng] += 1
        ev = (("e", eng), self.tick[eng])
        self.ops[eng].append((list(waits.items()), fn, ("e", eng)))
        self._commit(eng, waits, ev, reads, writes)
        return ev

    def dma(self, q, fn, reads=(), writes=()):
        i = self.dcnt[q] % NPOOL
        self.dcnt[q] += 1
        prev = self.duse[q][i]
        waits = self._collect(q, reads, writes, eng_is_dma=True)
        k = ("d", q, i)
        if prev > 0 and self.known[q].get(k, 0) < 16 * prev:
            waits[k] = 16 * prev
        self.duse[q][i] = prev + 1
        ev = (k, 16 * (prev + 1))
        self.ops[q].append((list(waits.items()), fn, k))
        self._commit(q, waits, ev, reads, writes)
        self.all_dma_ev[k] = 16 * (prev + 1)
        return ev

    def flush(self):
        waits = []
        for k, v in self.all_dma_ev.items():
            if self.known["sp"].get(k, 0) < v:
                waits.append((k, v))
        self.ops["sp"].append((waits, None, None))
        sched = self

        def run(name, e):
            for w, fn, inc in sched.ops[name]:
                for k, v in w:
                    e.wait_ge(sched._semof(k), v)
                if fn is None:
                    continue
                ins = fn(e)
                if inc[0] == "e":
                    ins.then_inc(sched.sem[name], 1)
                else:
                    ins.then_inc(sched._semof(inc), 16)

        with self.nc.Block() as block:
            @block.tensor
            def _(e):
                run("pe", e)

            @block.scalar
            def _(e):
                run("act", e)

            @block.vector
            def _(e):
                run("dve", e)

            @block.gpsimd
            def _(e):
                run("pool", e)

            @block.sync
            def _(e):
                run("sp", e)

        for e in ENG:
            self.ops[e] = []
            for e2 in ENG:
                self.known[e][("e", e2)] = self.tick[e2]
            for k, v in self.all_dma_ev.items():
                self.known[e][k] = v


class K:
    def __init__(self, nc, S):
        self.nc = nc
        self.S = S

    def mm(self, out, lhsT, rhs, start, stop, reads, writes):
        self.S.op("pe", lambda e: e.matmul(out, lhsT=lhsT, rhs=rhs, start=start, stop=stop,
                                           skip_group_check=True), reads, writes)

    def tr(self, out, in_, ident, reads, writes):
        self.S.op("pe", lambda e: e.transpose(out=out, in_=in_, identity=ident), reads, writes)

    def act(self, out, in_, func, reads, writes, scale=None, bias=None, accum=None):
        kw = {}
        if scale is not None:
            kw["scale"] = scale
        if bias is not None:
            kw["bias"] = bias
        if accum is not None:
            kw["accum_out"] = accum
        self.S.op("act", lambda e: e.activation(out=out, in_=in_, func=func, **kw), reads, writes)

    def tt(self, eng, out, in0, in1, op, reads, writes):
        self.S.op(eng, lambda e: e.tensor_tensor(out=out, in0=in0, in1=in1, op=op), reads, writes)

    def ts(self, eng, out, in0, s1, s2, op0, op1, reads, writes, accum=None):
        if op1 is None:
            self.S.op(eng, lambda e: e.tensor_scalar(out=out, in0=in0, scalar1=s1, scalar2=None, op0=op0),
                      reads, writes)
        elif accum is not None:
            self.S.op(eng, lambda e: e.tensor_scalar(out=out, in0=in0, scalar1=s1, scalar2=s2, op0=op0, op1=op1,
                                                     accum_out=accum), reads, writes)
        else:
            self.S.op(eng, lambda e: e.tensor_scalar(out=out, in0=in0, scalar1=s1, scalar2=s2, op0=op0, op1=op1),
                      reads, writes)

    def stt(self, out, in0, scalar, in1, op0, op1, reads, writes):
        self.S.op("dve", lambda e: e.scalar_tensor_tensor(out=out, in0=in0, scalar=scalar, in1=in1, op0=op0, op1=op1),
                  reads, writes)

    def cp(self, eng, out, in_, reads, writes):
        if eng == "act":
            self.act(out, in_, AF.Copy, reads, writes)
        else:
            self.S.op(eng, lambda e: e.tensor_copy(out=out, in_=in_), reads, writes)

    def memset(self, eng, ap, val, writes):
        self.S.op(eng, lambda e: e.memset(ap, val), (), writes)

    def recip(self, out, in_, reads, writes):
        self.S.op("dve", lambda e: e.reciprocal(out=out, in_=in_), reads, writes)

    def dma(self, q, out, in_, reads, writes, slow=False):
        if slow:
            self.S.dma(q, lambda e: e.dma_start(out=out, in_=in_, allow_slow_non_contiguous=True), reads, writes)
        else:
            self.S.dma(q, lambda e: e.dma_start(out=out, in_=in_), reads, writes)


def build_program():
    nc = bass.Bass("TRN2", target_bir_lowering=False)

    def din(name, shape, dt=F32):
        return nc.dram_tensor(name, list(shape), dt, kind="ExternalInput").ap()

    def dout(name, shape, dt=F32):
        return nc.dram_tensor(name, list(shape), dt, kind="ExternalOutput").ap()

    x_p = din("x_p", [TP, D])
    x_s = din("x_s", [TS, D])
    ck_d = din("ck", [PAST, 256])
    cv_d = din("cv", [PAST, 256])
    cki_d = din("cki", [PAST, 64])
    s0_d = din("s0", [4, 128, 128])
    c2_d = din("c2", [2, D])
    w_mod_d = din("w_mod", [D, 6 * D])
    b_mod_d = din("b_mod", [6 * D])
    norm1_d = din("norm1", [D])
    w_in_d = din("w_in", [D, 3656])
    lbl_d = din("lb_logits", [2, 512])
    gna_d = din("g_norm_a", [512])
    w_out_d = din("w_out", [D, D])
    norm2_d = din("norm2", [D])
    w_ff1_d = din("w_ff1", [D, 4 * D])
    w_ff2_d = din("w_ff2", [4 * D, D])
    normf_d = din("norm_f", [D])
    ident_d = din("c_ident", [128, 128])
    tri_d = din("c_tri", [128, 128])
    reset_d = din("c_reset", [128, 512])
    pow2_d = din("c_pow2", [128, NIT + 1])
    cos_d = din("c_cos", [128, 17, 8])
    sin_d = din("c_sin", [128, 17, 8])

    y_p = dout("y_p", [TP, D])
    y_s = dout("y_s", [TS, D])
    k_p = dout("k_p", [TP, 256])
    v_p = dout("v_p", [TP, 256])
    ki_p = dout("ki_p", [TP, 64])
    S_p = dout("S_p", [4, 128, 128])
    k_s = dout("k_s", [TS, 256])
    v_s = dout("v_s", [TS, 256])
    ki_s = dout("ki_s", [TS, 64])
    S_s = dout("S_s", [4, 128, 128])

    modS = nc.dram_tensor("modS", [2, 6 * D], F32, kind="Internal").ap()
    x1S = nc.dram_tensor("x1S", [NTOK, D], F32, kind="Internal").ap()
    r_modS = Res()
    r_x1S = [Res() for _ in range(17)]
    if DEBUG:
        dbg_x1 = dout("dbg_x1", [NTOK, D])
        dbg_mod = dout("dbg_mod", [2, 6 * D])
        dbg_z = dout("dbg_z", [128, NZ])
        dbg_mix = dout("dbg_mix", [128, 8, 128], BF16)
        dbg_acc = dout("dbg_acc", [128, NKEYMAX])

    with ExitStack() as st:
        S = Sched(nc, st)
        k = K(nc, S)

        def sbt(stack, name, shape, dt):
            return stack.enter_context(nc.sbuf_tensor(name, list(shape), dt))

        PS = st.enter_context(nc.psum_tensor("ps", [128, 4096], F32))
        RB = [Res("bank%d" % i, excl=True) for i in range(8)]

        def bank(b, ncols=512, off=0):
            return PS[:, b * 512 + off: b * 512 + off + ncols]

        identf = sbt(st, "identf", [128, 128], F32); r_identf = Res()
        identb = sbt(st, "identb", [128, 128], BF16); r_identb = Res()
        onesb = sbt(st, "onesb", [128, 128], BF16); r_onesb = Res()
        zb512 = sbt(st, "zb512", [128, 512], BF16); r_zb = Res()

        k.dma("sp", identf[:], ident_d, [], [r_identf])
        k.cp("dve", identb[:], identf[:], [r_identf], [r_identb])
        k.memset("dve", onesb[:], 1.0, [r_onesb])
        k.memset("dve", zb512[:], 0.0, [r_zb])

        with ExitStack() as p0:
            cT = sbt(p0, "cT", [128, 2, 8], F32); r_cT = Res()
            cE = sbt(p0, "cE", [128, 2, 8], F32); r_cE = Res()
            scT = sbt(p0, "scT", [128, 8, 2], BF16); r_scT = Res()
            wm = [sbt(p0, "wm%d" % i, [128, 8, 1024], BF16) for i in range(2)]
            r_wm = [Res(), Res()]
            bmod2 = sbt(p0, "bmod2", [2, 6 * D], F32); r_bmod2 = Res()
            modrow = sbt(p0, "modrow", [2, 6 * D], F32); r_modrow = Res()

            for g_ in range(2):
                k.dma("sp", cT[:, g_, :], c2_d[g_, :].rearrange("(c p) -> p c", p=128), [], [r_cT], slow=True)
            k.dma("sp", bmod2[:], b_mod_d.partition_broadcast(2), [], [r_bmod2])
            k.act(cE[:], cT[:], AF.Exp, [r_cT], [r_cE], scale=-1.0)
            k.ts("dve", cE[:], cE[:], 1.0, None, ALU.add, None, [r_cE], [r_cE])
            k.recip(cE[:], cE[:], [r_cE], [r_cE])
            k.tt("dve", scT[:].rearrange("p c g -> p g c"), cT[:], cE[:], ALU.mult, [r_cT, r_cE], [r_scT])
            wmv = w_mod_d.rearrange("(c p) n -> p c n", p=128)
            for m in range(6):
                b = m % 2
                for hh in range(2):
                    k.dma("pool", wm[b][:, :, hh * 512:(hh + 1) * 512],
                          wmv[:, :, m * 1024 + hh * 512: m * 1024 + (hh + 1) * 512], [], [r_wm[b]])
                for hh in range(2):
                    pb = (2 * m + hh) % 2
                    for c in range(8):
                        k.mm(PS[0:2, pb * 512: pb * 512 + 512], scT[:, c, :], wm[b][:, c, hh * 512:(hh + 1) * 512],
                             c == 0, c == 7, [r_scT, r_wm[b]], [RB[pb]])
                    cs = slice(m * 1024 + hh * 512, m * 1024 + (hh + 1) * 512)
                    k.tt("dve", modrow[:, cs], PS[0:2, pb * 512: pb * 512 + 512], bmod2[:, cs], ALU.add,
                         [RB[pb], r_bmod2], [r_modrow])
            k.dma("sp", modS, modrow[:], [r_modrow], [r_modS])
            if DEBUG:
                k.dma("sp", dbg_mod, modrow[:], [r_modrow], [])
            S.flush()

        with ExitStack() as pa:
            W_IN = sbt(pa, "W_IN", [128, 8, NZ], BF16); r_win = Res()
            W_OUT = sbt(pa, "W_OUT", [128, 8, D], BF16); r_wout = Res()
            zbiasB = sbt(pa, "zbiasB", [128, NZ], F32); r_zbias = Res()
            G1B = sbt(pa, "G1B", [128, D], F32); r_g1b = Res()
            n1c = sbt(pa, "n1c", [128, 8], F32); r_n1c = Res()
            sc1c = sbt(pa, "sc1c", [128, 8], F32); r_sc1c = Res()
            sh1c = sbt(pa, "sh1c", [128, 8], F32); r_sh1c = Res()
            gam1c = sbt(pa, "gam1c", [128, 8], F32); r_gam1c = Res()
            sh1bc = sbt(pa, "sh1bc", [128, 8, 128], BF16); r_sh1bc = Res()
            xt = [sbt(pa, "xt%d" % i, [128, D], F32) for i in range(2)]; r_xt = [Res(), Res()]
            smz = sbt(pa, "smz", [128, 4], F32); r_smz = Res()
            wtmp = sbt(pa, "wtmp", [128, 512], F32); r_wtmp = Res()
            hT = sbt(pa, "hT", [128, 8, 128], BF16); r_hT = Res()
            z = sbt(pa, "z", [128, NZ], F32); r_zA = Res(); r_zB = Res()
            rtmp = sbt(pa, "rtmp", [128, 4, NROPE * 8], F32); r_rtmp = Res()
            KT = sbt(pa, "KT", [128, 2, NKEYMAX], BF16); r_KT = Res()
            V1 = sbt(pa, "V1", [128, 17, 4, 65], BF16); r_V1 = Res()
            kIT = sbt(pa, "kIT", [128, NKEYMAX], BF16); r_kIT = Res()
            qz = sbt(pa, "qz", [128, 4, 2, 128], BF16); r_qz = Res()
            qIz = sbt(pa, "qIz", [128, 8, 128], BF16); r_qIz = Res()
            ONE_T = sbt(pa, "one_t", [128, 1], F32); r_one = Res()
            wI = sbt(pa, "wI", [128, 8], F32); r_wI = Res()
            nhwh = sbt(pa, "nhwh", [128, NIT + 1], F32)
            kbc = sbt(pa, "kbc", [128, 1], F32); r_kbc = Res()
            acc = sbt(pa, "acc", [128, NKEYMAX], F32); r_accb = [Res() for _ in range(5)]
            nmask = sbt(pa, "nmask", [128, NKEYMAX], BF16); r_nmask = Res()
            nmT = sbt(pa, "nmT", [128, 17, 128], BF16); r_nmT = Res()
            pT2 = [sbt(pa, "pT%d" % i, [128, 512], BF16) for i in range(2)]
            r_pT = [Res() for _ in range(2)]
            mixB = sbt(pa, "mixB", [128, 512], BF16); r_mixB = Res()
            mixTs = [sbt(pa, "mixT%d" % i, [128, 8, 128], BF16) for i in range(2)]; r_mixTs = [Res(), Res()]
            sm = sbt(pa, "sm", [128, 64], F32); r_sm = Res()
            hwtab = sbt(pa, "hwtab", [128, NIT + 1], F32); r_hwtab = Res()
            pow2 = sbt(pa, "pow2", [128, NIT + 1], F32); r_pow2 = Res()
            cosT = sbt(pa, "cosT", [128, 17, 8], F32); r_cos = Res()
            sinT = sbt(pa, "sinT", [128, 17, 8], F32)
            tri = sbt(pa, "tri", [128, 128], F32); r_tri = Res()
            resetm = sbt(pa, "resetm", [128, 512], F32); r_reset = Res()
            lbl = sbt(pa, "lbl", [128, 2, 4], F32); r_lbl = Res()
            lbc = sbt(pa, "lbc", [128, 4], F32); r_lbc = Res()
            omlb = sbt(pa, "omlb", [128, 4], F32)
            nomlb = sbt(pa, "nomlb", [128, 4], F32)
            gnc = sbt(pa, "gnc", [128, 4], F32); r_gnc = Res()
            Sf = sbt(pa, "Sf", [128, 4, 128], F32); r_Sf = Res()
            Sb = [sbt(pa, "Sb%d" % i, [128, 4, 128], BF16) for i in range(2)]
            r_Sb = [Res(), Res()]
            hq = sbt(pa, "hq", [128, 512], F32)
            hg = sbt(pa, "hg", [128, 512], F32)
            hlf = sbt(pa, "hlf", [128, 512], F32)
            hk = sbt(pa, "hk", [128, 512], F32)
            hb = sbt(pa, "hb", [128, 512], F32)
            heb = sbt(pa, "heb", [128, 512], F32)
            r_hg = Res(); r_hq = Res(); r_hlf = Res(); r_hk = Res(); r_hb = Res(); r_heb = Res()
            henb = hlf; r_henb = r_hlf
            qtl = sbt(pa, "qtl", [128, 512], BF16); r_qtl = Res()
            ktl = sbt(pa, "ktl", [128, 512], BF16); r_ktl = Res()
            kdT = sbt(pa, "kdT", [128, 512], BF16); r_kdT = Res()
            kdz = [sbt(pa, "kdz%d" % i, [128, 4, 128], BF16) for i in range(2)]
            r_kdz = [Res(), Res()]
            vA = sbt(pa, "vA", [128, 512], BF16); r_vA = Res()
            atm = sbt(pa, "atm", [128, 512], BF16); r_atm = Res()
            sqb = sbt(pa, "sqb", [128, 512], BF16); r_sqb = Res()
            cst = rtmp[:].rearrange("p a b -> p (a b)")[:, 0:640]; r_cst = r_rtmp

            k.dma("sp", tri[:], tri_d, [], [r_tri])
            k.dma("sp", resetm[:], reset_d, [], [r_reset])
            k.dma("sp", pow2[:], pow2_d, [], [r_pow2])
            k.dma("sp", cosT[:], cos_d, [], [r_cos])
            k.dma("sp", sinT[:], sin_d, [], [r_cos])
            k.dma("sp", lbl[:], lbl_d.rearrange("r (h p) -> p r h", p=128), [], [r_lbl], slow=True)
            k.dma("sp", gnc[:], gna_d.rearrange("(h p) -> p h", p=128), [], [r_gnc], slow=True)
            k.dma("sp", n1c[:], norm1_d.rearrange("(c p) -> p c", p=128), [], [r_n1c], slow=True)
            wiv = w_in_d.rearrange("(c p) n -> p c n", p=128)
            for (mc, oc, ln) in w_in_segments():
                k.dma("pool", W_IN[:, :, mc:mc + ln], wiv[:, :, oc:oc + ln], [], [r_win])
            wov = w_out_d.rearrange("(c p) n -> p c n", p=128)
            for hh in range(2):
                k.dma("pool", W_OUT[:, :, hh * 512:(hh + 1) * 512], wov[:, :, hh * 512:(hh + 1) * 512], [], [r_wout])
            k.tt("dve", lbc[:], lbl[:, 0, :], lbl[:, 1, :], ALU.subtract, [r_lbl], [r_lbc])
            k.act(lbc[:], lbc[:], AF.Exp, [r_lbc], [r_lbc], scale=-1.0)
            k.ts("dve", lbc[:], lbc[:], 1.0, None, ALU.add, None, [r_lbc], [r_lbc])
            k.recip(lbc[:], lbc[:], [r_lbc], [r_lbc])
            k.ts("dve", omlb[:], lbc[:], -1.0, 1.0, ALU.mult, ALU.add, [r_lbc], [r_lbc])
            k.ts("dve", nomlb[:], omlb[:], -1.0, None, ALU.mult, None, [r_lbc], [r_lbc])
            k.memset("pool", qz[:], 0.0, [r_qz])
            k.memset("pool", qIz[:], 0.0, [r_qIz])
            k.memset("dve", ONE_T[:], 1.0, [r_one])
            k.memset("pool", kdz[0][:], 0.0, [r_kdz[0]])
            k.memset("pool", kdz[1][:], 0.0, [r_kdz[1]])
            k.memset("pool", V1[:], 1.0, [r_V1])
            k.memset("dve", Sf[:], 0.0, [r_Sf])
            k.memset("dve", Sb[0][:], 0.0, [r_Sb[0]])

            def prep_group(g):
                k.dma("sp", sh1c[:], modS[g, 0:D].rearrange("(c p) -> p c", p=128), [r_modS], [r_sh1c], slow=True)
                k.dma("sp", sc1c[:], modS[g, D:2 * D].rearrange("(c p) -> p c", p=128), [r_modS], [r_sc1c], slow=True)
                k.dma("sp", G1B[:], modS[g, 2 * D:3 * D].partition_broadcast(128), [r_modS], [r_g1b])
                k.ts("dve", gam1c[:], sc1c[:], 1.0, None, ALU.add, None, [r_sc1c], [r_gam1c])
                k.tt("dve", gam1c[:], gam1c[:], n1c[:], ALU.mult, [r_gam1c, r_n1c], [r_gam1c])
                k.cp("dve", sh1bc[:], sh1c[:].unsqueeze(2).broadcast_to([128, 8, 128]), [r_sh1c], [r_sh1bc])
                for bi, (c0, ncol) in enumerate(ZBLOCKS):
                    pb = 2 + bi % 2
                    for c in range(8):
                        k.mm(bank(pb, ncol), sh1bc[:, c, :], W_IN[:, c, c0:c0 + ncol], c == 0, c == 7,
                             [r_sh1bc, r_win], [RB[pb]])
                    k.cp("act", zbiasB[:, c0:c0 + ncol], bank(pb, ncol), [RB[pb]], [r_zbias])

            def zstage(g, ti):
                P = 128 if g == 0 else 64
                t0 = ti * 128 if g == 0 else 0
                xsrc = x_p if g == 0 else x_s
                tix = ti if g == 0 else 16
                xb_ = xt[tix % 2]
                rx = r_xt[tix % 2]
                k.dma("sp", xb_[:P, :], xsrc[t0:t0 + P, :], [], [rx])
                k.act(wtmp[:P, :], xb_[:P, 0:512], AF.Square, [rx], [r_wtmp, r_smz], accum=smz[:P, 0:1])
                k.act(wtmp[:P, :], xb_[:P, 512:1024], AF.Square, [rx], [r_wtmp, r_smz], accum=smz[:P, 3:4])
                k.ts("dve", smz[:P, 3:4], smz[:P, 3:4], 1.0 / D, EPS, ALU.mult, ALU.add, [r_smz], [r_smz])
                k.act(smz[:P, 1:2], smz[:P, 0:1], AF.Ln, [r_smz], [r_smz], scale=1.0 / D, bias=smz[:P, 3:4])
                k.act(smz[:P, 2:3], smz[:P, 1:2], AF.Exp, [r_smz], [r_smz], scale=-0.5)
                rstd = smz[:P, 2:3]
                yield
                xTp = PS[:, 0:1024].rearrange("p (c t) -> p c t", c=8)
                for c in range(8):
                    k.tr(xTp[:, c, :P], xb_[:P, c * 128:(c + 1) * 128], identf[:P, :P], [rx, r_identf],
                         [RB[0] if c < 4 else RB[1]])
                yield
                k.tt("dve", hT[:, :, :P], xTp[:, :, :P], gam1c[:, :].unsqueeze(2).broadcast_to([128, 8, P]), ALU.mult,
                     [RB[0], RB[1], r_gam1c], [r_hT])
                yield
                for bi, (c0, ncol) in enumerate(ZBLOCKS):
                    pb = 2 + bi % 2
                    for c in range(8):
                        k.mm(PS[:P, pb * 512: pb * 512 + ncol], hT[:, c, :P], W_IN[:, c, c0:c0 + ncol], c == 0, c == 7,
                             [r_hT, r_win], [RB[pb]])
                    k.stt(z[:P, c0:c0 + ncol], PS[:P, pb * 512: pb * 512 + ncol], rstd, zbiasB[:P, c0:c0 + ncol],
                          ALU.mult, ALU.add, [RB[pb], r_smz, r_zbias], [r_zA if c0 < 2048 else r_zB])
                    yield
                zr = z[:P, Z_QB:Z_QB + NROPE * 64].rearrange("p (h d) -> p h d", d=64)
                x1v = zr[:, :, 0:8]
                x2v = zr[:, :, 8:16]
                cb_ = cosT[:P, tix, :].unsqueeze(1).broadcast_to([P, NROPE, 8])
                sb_ = sinT[:P, tix, :].unsqueeze(1).broadcast_to([P, NROPE, 8])
                tv = [rtmp[:P, j, :].rearrange("p (h d) -> p h d", d=8) for j in range(4)]
                k.tt("pool", tv[0], x1v, cb_, ALU.mult, [r_zB, r_cos], [r_rtmp])
                k.tt("pool", tv[1], x2v, sb_, ALU.mult, [r_zB, r_cos], [r_rtmp])
                k.tt("pool", tv[2], x2v, cb_, ALU.mult, [r_zB, r_cos], [r_rtmp])
                k.tt("pool", tv[3], x1v, sb_, ALU.mult, [r_zB, r_cos], [r_rtmp])
                k.tt("pool", x1v, tv[0], tv[1], ALU.subtract, [r_rtmp], [r_zB])
                k.tt("pool", x2v, tv[2], tv[3], ALU.add, [r_rtmp], [r_zB])
                yield
                ko, vo, kio = (k_p, v_p, ki_p) if g == 0 else (k_s, v_s, ki_s)
                k.dma("sp", ko[t0:t0 + P, :], z[:P, Z_KB:Z_KB + 256], [r_zB], [])
                k.dma("sp", vo[t0:t0 + P, :], z[:P, Z_VB:Z_VB + 256], [r_zB], [])
                k.dma("sp", kio[t0:t0 + P, :], z[:P, Z_KI:Z_KI + 64], [r_zB], [])
                if DEBUG and g == 0 and ti == 2:
                    k.dma("sp", dbg_z, z[:, :], [r_zA, r_zB], [])

            def drain(gen):
                for _ in gen:
                    pass

            def tile_params(g, ti):
                P = 128 if g == 0 else 64
                tix = ti if g == 0 else 16
                v3 = lambda t: t[:, 0:4 * P].rearrange("p (h t) -> p h t", h=4)
                return P, tix, v3

            def hgrn_of(g, ti):
                P, tix, v3 = tile_params(g, ti)
                return hgrn(g, ti, P, v3, mixTs[tix % 2], r_mixTs[tix % 2])

            def tileA(g, ti, nxt=None, hg_cur=None, hg_nxt=None):
                P, tix, v3 = tile_params(g, ti)
                t0 = ti * 128 if g == 0 else 0
                key0 = tix * 128
                nk = 128 * (ti + 1) if g == 0 else NKEYMAX
                nkb = nk // 128
                xb_ = xt[tix % 2]
                rx = r_xt[tix % 2]
                mixT = mixTs[tix % 2]
                r_mixT = r_mixTs[tix % 2]
                gd = dsa(g, ti, P, tix, key0, nk, nkb, mixT, r_mixT)

                def rr(gens, stop_mark):
                    hit = False
                    while gens and not hit:
                        for gen in list(gens):
                            try:
                                if next(gen) == stop_mark:
                                    hit = True
                            except StopIteration:
                                gens.remove(gen)

                rr([gd] + ([hg_cur] if hg_cur is not None else []), "BISECT")
                if hg_cur is not None:
                    drain(hg_cur)
                rr([gd] + ([nxt] if nxt is not None else []), "BISECT_END")
                if nxt is not None:
                    drain(nxt)
                rr([gd] + ([hg_nxt] if hg_nxt is not None else []), "NEVER")
                if DEBUG and g == 0 and ti == 2:
                    k.dma("sp", dbg_mix, mixT[:], [r_mixT], [])
                for hh in range(2):
                    pb = 4 + hh
                    for c in range(8):
                        k.mm(PS[:P, pb * 512:(pb + 1) * 512], mixT[:, c, :P], W_OUT[:, c, hh * 512:(hh + 1) * 512],
                             c == 0, c == 7, [r_mixT, r_wout], [RB[pb]])
                    k.tt("dve", wtmp[:P, :], PS[:P, pb * 512:(pb + 1) * 512],
                         G1B[:P, hh * 512:(hh + 1) * 512], ALU.mult, [RB[pb], r_g1b], [r_wtmp])
                    k.tt("dve", xb_[:P, hh * 512:(hh + 1) * 512], wtmp[:P, :],
                         xb_[:P, hh * 512:(hh + 1) * 512], ALU.add, [r_wtmp, rx], [rx])
                tok0 = t0 if g == 0 else TP
                k.dma("sp", x1S[tok0:tok0 + P, :], xb_[:P, :], [rx], [r_x1S[tix]])

            def hgrn(g, ti, P, v3, mixT, r_mixT):
                N4 = 4 * P
                hp = PS[:, 2048:2048 + 12 * 128].rearrange("p (j t) -> p j t", j=12)
                for j in range(12):
                    grp = j // 4
                    c0 = (Z_QA, Z_FA, Z_GA)[grp] + (j % 4) * 128
                    k.tr(hp[:, j, :P], z[:P, c0:c0 + 128], identf[:P, :P], [r_zA, r_identf], [RB[4 + grp]])
                yield
                k.cp("pool", vA[:P, :], z[:P, Z_IA:Z_IA + 512], [r_zA], [r_vA])
                Eqf = z[:, 0:8 * P].rearrange("p (j t) -> p j t", j=8)
                Eg = z[:, 1536:1536 + 4 * P].rearrange("p (j t) -> p j t", j=4)
                k.act(Eqf[:, 0:4, :], hp[:, 0:4, :P], AF.Exp, [RB[4]], [r_zA], scale=-1.0)
                k.act(Eqf[:, 4:8, :], hp[:, 4:8, :P], AF.Exp, [RB[5]], [r_zA], scale=-1.0)
                k.act(Eg, hp[:, 8:12, :P], AF.Exp, [RB[6]], [r_zA], scale=-1.0)
                yield
                ff = z[:, 4 * P:8 * P]
                k.ts("dve", ff, ff, 1.0, None, ALU.add, None, [r_zA], [r_zA])
                k.recip(ff, ff, [r_zA], [r_zA])
                yield
                for fl in (z[:, 0:4 * P], z[:, 1536:1536 + 4 * P]):
                    k.act(fl, fl, AF.Ln, [r_zA, r_one], [r_zA], bias=ONE_T[:, :])
                    k.act(fl, fl, AF.Exp, [r_zA], [r_zA], scale=-1.0)
                    yield
                sigq, sigf, sigg = Eqf[:, 0:4, :], Eqf[:, 4:8, :], Eg
                k.tt("dve", v3(hq), hp[:, 0:4, :P], sigq, ALU.mult, [RB[4], r_zA], [r_hq])
                yield
                for h in range(4):
                    k.stt(v3(hg)[:, h, :], hp[:, 8 + h, :P], gnc[:, h:h + 1], sigg[:, h, :], ALU.mult, ALU.mult,
                          [RB[6], r_zA, r_gnc], [r_hg])
                yield
                for h in range(4):
                    k.act(v3(hlf)[:, h, :], sigf[:, h, :], AF.Ln, [r_zA, r_lbc], [r_hlf],
                          scale=omlb[:, h:h + 1], bias=lbc[:, h:h + 1])
                for h in range(4):
                    k.ts("dve", v3(hk)[:, h, :], sigf[:, h, :], nomlb[:, h:h + 1], omlb[:, h:h + 1], ALU.mult, ALU.add,
                         [r_zA, r_lbc], [r_hk])
                yield
                S.op("dve", lambda e: e.tensor_tensor_scan(out=hb[:, 0:N4], data0=resetm[:, 0:N4], data1=hlf[:, 0:N4],
                                                            initial=0.0, op0=ALU.mult, op1=ALU.add),
                     [r_hlf, r_reset], [r_hb])
                k.act(heb[:, 0:N4], hb[:, 0:N4], AF.Exp, [r_hb], [r_heb])
                k.act(henb[:, 0:N4], hb[:, 0:N4], AF.Exp, [r_hb], [r_henb], scale=-1.0)
                yield
                k.tt("dve", qtl[:, 0:N4], hq[:, 0:N4], heb[:, 0:N4], ALU.mult, [r_hq, r_heb], [r_qtl])
                k.tt("dve", ktl[:, 0:N4], hk[:, 0:N4], henb[:, 0:N4], ALU.mult, [r_hk, r_henb], [r_ktl])
                yield
                nck = P // 64
                for h in range(4):
                    for ck in range(nck):
                        le = ck * 64 + 63
                        k.act(v3(henb)[:, h, ck * 64:ck * 64 + 64], v3(hb)[:, h, ck * 64:ck * 64 + 64], AF.Exp,
                              [r_hb], [r_henb], scale=-1.0, bias=v3(hb)[:, h, le:le + 1])
                ab = PS[:, 7 * 512: 7 * 512 + 512]
                abv = ab[:, 0:N4].rearrange("p (h t) -> p h t", h=4)
                for h in range(4):
                    k.mm(abv[:P, h, :], v3(ktl)[:, h, :], v3(qtl)[:, h, :], True, True, [r_ktl, r_qtl], [RB[7]])
                yield
                k.tt("dve", kdT[:, 0:N4], hk[:, 0:N4], henb[:, 0:N4], ALU.mult, [r_hk, r_henb], [r_kdT])
                k.tt("dve", v3(atm)[:P], abv[:P], tri[:P, :P].unsqueeze(1).broadcast_to([P, 4, P]), ALU.mult,
                     [RB[7], r_tri], [r_atm])
                yield
                kp = PS[:, 7 * 512: 7 * 512 + 256].bitcast(BF16).rearrange("p (h d) -> p h d", h=4)
                for h in range(4):
                    k.tr(kp[:P, h, :], v3(kdT)[:, h, :], identb[:, :], [r_kdT, r_identb], [RB[7]])
                for ck in range(nck):
                    k.cp("act", kdz[ck][ck * 64:ck * 64 + 64, :, :], kp[ck * 64:ck * 64 + 64, :, :], [RB[7]], [r_kdz[ck]])
                yield
                ob = PS[:, 4 * 512: 4 * 512 + 512]
                obv = ob[:, 0:N4].rearrange("p (h t) -> p h t", h=4)
                k.mm(ob[:, 0:512], zb512[:, 0:128], zb512[:, 0:512], True, False, [r_zb], [RB[4]])
                for ck in range(nck):
                    sub = PS[:, (5 + ck) * 512: (5 + ck) * 512 + 512].rearrange("p (h d) -> p h d", h=4)
                    for h in range(4):
                        k.mm(sub[:, h, :], kdz[ck][:P, h, :], vA[:P, h * 128:(h + 1) * 128], True, True,
                             [r_kdz[ck], r_vA], [RB[5 + ck]])
                for h in range(4):
                    k.mm(obv[:, h, :], vA[:P, h * 128:(h + 1) * 128], v3(atm)[:P, h, :], False, False,
                         [r_vA, r_atm], [RB[4]])
                yield
                cur = hgrn_state["cur"]
                for ck in range(nck):
                    le = ck * 64 + 63
                    sub = PS[:, (5 + ck) * 512: (5 + ck) * 512 + 512].rearrange("p (h d) -> p h d", h=4)
                    for h in range(4):
                        k.mm(obv[:, h, ck * 64:ck * 64 + 64], Sb[cur][:, h, :], v3(qtl)[:, h, ck * 64:ck * 64 + 64],
                             False, False, [r_Sb[cur], r_qtl], [RB[4]])
                    nxt = 1 - cur
                    for h in range(4):
                        k.stt(Sf[:, h, :], Sf[:, h, :], v3(heb)[:, h, le:le + 1], sub[:, h, :], ALU.mult, ALU.add,
                              [r_Sf, r_heb, RB[5 + ck]], [r_Sf])
                    k.cp("act", Sb[nxt][:], Sf[:], [r_Sf], [r_Sb[nxt]])
                    cur = nxt
                    yield
                hgrn_state["cur"] = cur
                k.act(sqb[:, 0:N4], ob[:, 0:N4], AF.Square, [RB[4]], [r_sqb])
                k.mm(PS[:, 7 * 512: 7 * 512 + N4], onesb[:, :], sqb[:, 0:N4], True, True, [r_onesb, r_sqb], [RB[7]])
                k.act(hq[:, 0:N4], PS[:, 7 * 512: 7 * 512 + N4], AF.Ln, [RB[7]], [r_hq], scale=1.0 / 128, bias=EPS_AP[:, :])
                k.act(hq[:, 0:N4], hq[:, 0:N4], AF.Exp, [r_hq], [r_hq], scale=-0.5)
                yield
                k.tt("dve", hq[:, 0:N4], ob[:, 0:N4], hq[:, 0:N4], ALU.mult, [RB[4], r_hq], [r_hq])
                k.tt("dve", mixT[:, 0:4, :P], v3(hq), v3(hg), ALU.mult, [r_hq, r_hg], [r_mixT])

            def dsa(g, ti, P, tix, key0, nk, nkb, mixT, r_mixT):
                dp = PS[:, 0:11 * 128].rearrange("p (j t) -> p j t", j=11)
                srcs = [Z_QB + 128 * j for j in range(4)] + [Z_KB, Z_KB + 128] + [Z_QI + 128 * j for j in range(4)] + [Z_KI]
                for j, c0 in enumerate(srcs):
                    k.tr(dp[:, j, :P], z[:P, c0:c0 + 128], identf[:P, :P], [r_zB, r_identf], [RB[j // 4]])
                yield
                qzv = qz[:].rearrange("p (pr e) g t -> p pr e g t", e=2)
                qIzv = qIz[:].rearrange("p (j e) t -> p j e t", e=2)
                for e_ in range(2):
                    ps_ = slice(64 * e_, 64 * e_ + 64)
                    k.cp("act", qzv[ps_, :, e_, :, :P], dp[ps_, 0:4, :P].rearrange("p (pr g) t -> p pr g t", g=2),
                         [RB[0]], [r_qz])
                    k.cp("act", qIzv[ps_, :, e_, :P], dp[ps_, 6:10, :P], [RB[1], RB[2]], [r_qIz])
                k.cp("dve", KT[:, :, key0:key0 + P], dp[:, 4:6, :P], [RB[1]], [r_KT])
                k.cp("dve", kIT[:, key0:key0 + P], dp[:, 10, :P], [RB[2]], [r_kIT])
                k.cp("pool", wI[:P, :], z[:P, Z_WI:Z_WI + 8], [r_zB], [r_wI])
                k.cp("pool", V1[:P, tix, :, 0:64], z[:P, Z_VB:Z_VB + 256].rearrange("p (n d) -> p n d", d=64),
                     [r_zB], [r_V1])
                yield
                cnt = 0
                r_acc = r_accb[0:(nk + 511) // 512]
                for h in range(8):
                    for kb0 in range(0, nk, 512):
                        ncol = min(512, nk - kb0)
                        ra = r_accb[kb0 // 512]
                        pb = cnt % 4
                        cnt += 1
                        k.mm(PS[:P, pb * 512: pb * 512 + ncol], qIz[:, h, :P], kIT[:, kb0:kb0 + ncol], True, True,
                             [r_qIz, r_kIT], [RB[pb]])
                        k.act(PS[:P, pb * 512: pb * 512 + ncol], PS[:P, pb * 512: pb * 512 + ncol], AF.Relu,
                              [RB[pb]], [RB[pb]])
                        wcol = wI[:P, h:h + 1]
                        if h == 0:
                            k.ts("dve", acc[:P, kb0:kb0 + ncol], PS[:P, pb * 512: pb * 512 + ncol], wcol, None,
                                 ALU.mult, None, [RB[pb], r_wI], [ra])
                        else:
                            k.stt(acc[:P, kb0:kb0 + ncol], PS[:P, pb * 512: pb * 512 + ncol], wcol,
                                  acc[:P, kb0:kb0 + ncol], ALU.mult, ALU.add, [RB[pb], r_wI, ra], [ra])
                        yield
                thr = sm[:P, 8:9]
                nreal = nk if g == 0 else 2112
                yield "BISECT"
                if g == 1:
                    k.memset("dve", acc[:P, 2112:NKEYMAX], -1e30, r_acc)
                if nk <= 256:
                    if g == 0:
                        k.memset("dve", acc[0:64, nk - 64:nk], -1e30, r_acc)
                    k.memset("dve", thr, -1e29, [r_sm])
                else:
                    S.op("dve", lambda e: e.tensor_reduce(out=sm[:P, 4:5], in_=acc[:P, 0:nreal], axis=AX.X, op=ALU.max),
                         r_acc, [r_sm])
                    S.op("dve", lambda e: e.tensor_reduce(out=sm[:P, 5:6], in_=acc[:P, 0:nreal], axis=AX.X, op=ALU.min),
                         r_acc, [r_sm])
                    if g == 0:
                        k.memset("dve", acc[0:64, nk - 64:nk], -1e30, r_acc)
                    k.memset("dve", kbc[:P, :], float(nk - 511), [r_kbc])
                    k.tt("dve", sm[:P, 6:7], sm[:P, 4:5], sm[:P, 5:6], ALU.subtract, [r_sm], [r_sm])
                    k.ts("dve", sm[:P, 6:7], sm[:P, 6:7], 1.0001, 1e-6, ALU.mult, ALU.add, [r_sm], [r_sm])
                    k.ts("dve", nhwh[:P, :], pow2[:P, :], sm[:P, 6:7], None, ALU.mult, None, [r_pow2, r_sm], [r_hwtab])
                    k.ts("dve", sm[:P, 11:12], sm[:P, 6:7], -0.5, None, ALU.mult, None, [r_sm], [r_sm])
                    negmid = sm[:P, 7:8]
                    k.stt(negmid, sm[:P, 5:6], -1.0, sm[:P, 11:12], ALU.mult, ALU.add, [r_sm], [r_sm])
                    yield
                    for n in range(NIT):
                        k.act(nmask[:P, 0:nk], acc[:P, 0:nk], AF.Sign, r_acc + [r_sm], [r_nmask, r_sm],
                              bias=negmid, accum=sm[:P, 9:10])
                        k.ts("dve", sm[:P, 10:11], sm[:P, 9:10], float(511.5 - nk), -0.5, ALU.is_ge, ALU.add,
                             [r_sm], [r_sm])
                        k.stt(negmid, sm[:P, 10:11], nhwh[:P, n:n + 1], negmid, ALU.mult, ALU.add, [r_sm, r_hwtab], [r_sm])
                        yield
                    k.stt(thr, negmid, -1.0, nhwh[:P, NIT:NIT + 1], ALU.mult, ALU.add, [r_sm, r_hwtab], [r_sm])
                yield "BISECT_END"
                k.ts("dve", nmask[:P, 0:nk], acc[:P, 0:nk], thr, NEG, ALU.is_lt, ALU.mult, r_acc + [r_sm], [r_nmask])
                yield
                if DEBUG and g == 0 and ti == 2:
                    k.dma("sp", dbg_acc[:, 0:nk], acc[:, 0:nk], r_acc, [])
                for q0 in range(0, nkb, 4):
                    nq = min(4, nkb - q0)
                    pbk = 2 + (q0 // 4) % 2
                    tp = PS[:, pbk * 512: pbk * 512 + 256].bitcast(BF16).rearrange("p (j t) -> p j t", j=4)
                    for j in range(nq):
                        kb = q0 + j
                        k.tr(tp[:, j, :P], nmask[:P, kb * 128:(kb + 1) * 128], identb[:P, :P], [r_nmask, r_identb],
                             [RB[pbk]])
                    k.cp("act", nmT[:, q0:q0 + nq, :P], tp[:, 0:nq, :P], [RB[pbk]], [r_nmT])
                    yield
                oacc = [PS[:, (2 + gg) * 512: (2 + gg) * 512 + 260].rearrange("p (n d) -> p n d", n=4) for gg in range(2)]
                for gg in range(2):
                    k.mm(PS[:P, (2 + gg) * 512:(2 + gg) * 512 + 512], zb512[:, 0:P], zb512[:, 0:512], True, False,
                         [r_zb], [RB[2 + gg]])
                steps = [(n, kb, min(2, nkb - kb)) for n in range(4) for kb in range(0, nkb, 2)]

                def lg_of(i, w):
                    pb = i % 2
                    return PS[:, pb * 512: pb * 512 + w * 2 * P]

                def emit_lg(i):
                    n, kb, w = steps[i]
                    blk = n // 2
                    for j in range(w):
                        lgv = PS[:, (i % 2) * 512 + j * 2 * P: (i % 2) * 512 + (j + 1) * 2 * P].rearrange(
                            "p (g t) -> p g t", g=2)
                        k.mm(lgv, KT[:, blk, (kb + j) * 128:(kb + j + 1) * 128], qz[:, n, :, :P], True, False,
                             [r_KT, r_qz], [RB[i % 2]])
                        k.mm(lgv, identb[:, :], nmT[:, kb + j, :P].unsqueeze(1).broadcast_to([128, 2, P]), False, True,
                             [r_identb, r_nmT], [RB[i % 2]])

                emit_lg(0)
                for i, (n, kb, w) in enumerate(steps):
                    if i + 1 < len(steps):
                        emit_lg(i + 1)
                    slot = i % 2
                    k.act(pT2[slot][:, 0:w * 2 * P], lg_of(i, w), AF.Exp, [RB[i % 2]], [r_pT[slot]], scale=0.125)
                    for j in range(w):
                        for gg in range(2):
                            k.mm(oacc[gg][:P, n, :], pT2[slot][:, j * 2 * P + gg * P: j * 2 * P + (gg + 1) * P],
                                 V1[:, kb + j, n, :], False, False, [r_pT[slot], r_V1], [RB[2 + gg]])
                    yield
                mbv = mixB[:, :].rearrange("p (n g d) -> p n g d", n=4, g=2)
                for gg in range(2):
                    k.recip(sm[:P, 16 + 4 * gg:20 + 4 * gg], oacc[gg][:P, :, 64], [RB[2 + gg]], [r_sm])
                    k.tt("dve", mbv[:P, :, gg, :], oacc[gg][:P, :, 0:64],
                         sm[:P, 16 + 4 * gg:20 + 4 * gg].unsqueeze(2).broadcast_to([P, 4, 64]), ALU.mult,
                         [RB[2 + gg], r_sm], [r_mixB])
                yield
                mp = PS[:, 0:256].bitcast(BF16).rearrange("p (j t) -> p j t", j=4)
                for j in range(4):
                    k.tr(mp[:, j, :P], mixB[:P, j * 128:(j + 1) * 128], identb[:P, :P], [r_mixB, r_identb], [RB[0]])
                k.cp("act", mixT[:, 4:8, :P], mp[:, :, :P], [RB[0]], [r_mixT])

            def sample_cache_prep():
                k.memset("dve", KT[:, :, 2112:NKEYMAX], 0.0, [r_KT])
                k.memset("dve", kIT[:, 2112:NKEYMAX], 0.0, [r_kIT])
                k.cp("pool", V1[64:128, 16, :, 0:64], zb512[64:128, 0:256].rearrange("p (n d) -> p n d", d=64),
                     [r_zb], [r_V1])
                cp_ = PS[:, 2048:2048 + 3 * 128].rearrange("p (j t) -> p j t", j=3)
                for ct in range(16):
                    k.dma("sp", cst[:, 0:256], ck_d[ct * 128:(ct + 1) * 128, :], [], [r_cst])
                    k.dma("sp", cst[:, 256:512], cv_d[ct * 128:(ct + 1) * 128, :], [], [r_cst])
                    k.dma("sp", cst[:, 512:576], cki_d[ct * 128:(ct + 1) * 128, :], [], [r_cst])
                    k.cp("dve", cst[:, 576:640], cst[:, 512:576], [r_cst], [r_cst])
                    for j, c0 in enumerate((0, 128, 512)):
                        k.tr(cp_[:, j, :], cst[:, c0:c0 + 128], identf[:, :], [r_cst, r_identf], [RB[4]])
                    k.cp("act", KT[:, :, ct * 128:(ct + 1) * 128], cp_[:, 0:2, :], [RB[4]], [r_KT])
                    k.cp("dve", kIT[:, ct * 128:(ct + 1) * 128], cp_[:, 2, :], [RB[4]], [r_kIT])
                    k.cp("pool", V1[:, ct, :, 0:64], cst[:, 256:512].rearrange("p (n d) -> p n d", d=64), [r_cst], [r_V1])

            EPS_T = sbt(pa, "eps_t", [128, 1], F32); r_eps = Res()
            EPS_AP = EPS_T
            k.memset("dve", EPS_T[:], EPS, [r_eps])
            hgrn_state = {"cur": 0}

            if CFG["A"]:
                prep_group(0)
                nt_ = CFG["ntiles"]
                drain(zstage(0, 0))
                for ti in range(nt_):
                    last = ti + 1 >= nt_
                    tileA(0, ti, None if last else zstage(0, ti + 1), hgrn_of(0, 0) if ti == 0 else None,
                          None if last else hgrn_of(0, ti + 1))
                k.dma("sp", S_p.rearrange("h k v -> k h v"), Sf[:], [r_Sf], [])
            if CFG["A"] and CFG["sample"]:
                k.dma("sp", Sf[:], s0_d.rearrange("h k v -> k h v"), [], [r_Sf])
                k.cp("act", Sb[hgrn_state["cur"]][:], Sf[:], [r_Sf], [r_Sb[hgrn_state["cur"]]])
                if CFG["scp"]:
                    sample_cache_prep()
                prep_group(1)
                drain(zstage(1, 0))
                tileA(1, 0, None, hgrn_of(1, 0), None)
                k.dma("sp", S_s.rearrange("h k v -> k h v"), Sf[:], [r_Sf], [])
            if DEBUG:
                k.dma("sp", dbg_x1, x1S, r_x1S, [])
            S.flush()

        with ExitStack() as pb_:
            W1 = sbt(pb_, "W1", [128, 8, 4 * D], BF16); r_w1 = [Res() for _ in range(8)]
            W2 = sbt(pb_, "W2", [128, 32, D], BF16); r_w2 = [Res() for _ in range(8)]
            GAM2B = sbt(pb_, "GAM2B", [128, D], F32); r_gam2 = Res()
            SH2B = sbt(pb_, "SH2B", [128, D], F32); r_sh2 = Res()
            G2B = sbt(pb_, "G2B", [128, D], F32); r_g2 = Res()
            NFB = sbt(pb_, "NFB", [128, D], F32); r_nf = Res()
            x1t = [sbt(pb_, "x1t%d" % i, [128, D], F32) for i in range(4)]
            r_x1t = [Res() for _ in range(4)]
            tmpf = sbt(pb_, "tmpf", [128, D], F32); r_tmpf = Res()
            tmpg = sbt(pb_, "tmpg", [128, D], F32); r_tmpg = Res()
            h2 = sbt(pb_, "h2", [128, D], BF16); r_h2 = Res()
            h2T = [sbt(pb_, "h2T%d" % i, [128, 8, 256], BF16) for i in range(2)]
            r_h2T = [Res(), Res()]
            smb2 = sbt(pb_, "smb2", [128, 16], F32); r_smb2 = Res()
            rr = [sbt(pb_, "rr%d" % i, [128, 256], F32) for i in range(2)]
            r_rr = [Res(), Res()]
            uT = sbt(pb_, "uT", [128, 32, 256], BF16); r_uT = Res()
            smb = sbt(pb_, "smb", [128, 16], F32); r_smb = Res()
            epsb = sbt(pb_, "epsb", [128, 1], F32); r_epsb = Res()
            k.memset("dve", epsb[:], EPS, [r_epsb])

            w1v = w_ff1_d.rearrange("(c p) n -> p c n", p=128)
            for j in range(8):
                k.dma("pool", W1[:, :, j * 512:(j + 1) * 512], w1v[:, :, j * 512:(j + 1) * 512], [], [r_w1[j]])
            w2v = w_ff2_d.rearrange("(c p) n -> p c n", p=128)
            for j in range(8):
                k.dma("pool", W2[:, 4 * j:4 * j + 4, :], w2v[:, 4 * j:4 * j + 4, :], [], [r_w2[j]])
            k.dma("sp", NFB[:], normf_d.partition_broadcast(128), [], [r_nf])

            def prepB(g):
                k.dma("sp", SH2B[:], modS[g, 3 * D:4 * D].partition_broadcast(128), [], [r_sh2])
                k.dma("sp", GAM2B[:], modS[g, 4 * D:5 * D].partition_broadcast(128), [], [r_gam2])
                k.dma("sp", G2B[:], modS[g, 5 * D:6 * D].partition_broadcast(128), [], [r_g2])
                k.dma("sp", tmpf[:], norm2_d.partition_broadcast(128), [], [r_tmpf])
                k.ts("dve", GAM2B[:], GAM2B[:], 1.0, None, ALU.add, None, [r_gam2], [r_gam2])
                k.tt("dve", GAM2B[:], GAM2B[:], tmpf[:], ALU.mult, [r_gam2, r_tmpf], [r_gam2])

            def prologueB(g, bi, par):
                TB = 256 if g == 0 else 64
                tok0 = bi * 256 if g == 0 else TP
                ntl = 2 if g == 0 else 1
                P = 128 if g == 0 else 64
                h2Tp = PS[:, 0:1024].bitcast(BF16).rearrange("p (c t) -> p c t", c=8)
                for tl in range(ntl):
                    xb_ = x1t[2 * par + tl]
                    rx = r_x1t[2 * par + tl]
                    k.dma("sp", xb_[:P, :], x1S[tok0 + tl * 128: tok0 + tl * 128 + P, :], [], [rx])
                    k.act(tmpg[:P, :], xb_[:P, :], AF.Square, [rx], [r_tmpg, r_smb], accum=smb[:P, 0:1])
                    k.act(smb[:P, 1:2], smb[:P, 0:1], AF.Ln, [r_smb, r_epsb], [r_smb], scale=1.0 / D, bias=epsb[:P, :])
                    k.act(smb[:P, 2:3], smb[:P, 1:2], AF.Exp, [r_smb], [r_smb], scale=-0.5)
                    k.stt(tmpg[:P, :], xb_[:P, :], smb[:P, 2:3], GAM2B[:P, :], ALU.mult, ALU.mult,
                          [rx, r_smb, r_gam2], [r_tmpg])
                    k.tt("pool", h2[:P, :], tmpg[:P, :], SH2B[:P, :], ALU.add, [r_tmpg, r_sh2], [r_h2])
                    for c in range(8):
                        k.tr(h2Tp[:, c, tl * 128: tl * 128 + P], h2[:P, c * 128:(c + 1) * 128], identb[:P, :P],
                             [r_h2, r_identb], [RB[0] if c < 4 else RB[1]])
                k.cp("act", h2T[par][:, 0:4, 0:TB], h2Tp[:, 0:4, 0:TB], [RB[0]], [r_h2T[par]])
                k.cp("dve", h2T[par][:, 4:8, 0:TB], h2Tp[:, 4:8, 0:TB], [RB[1]], [r_h2T[par]])

            def ffn1B(g, bi, par):
                TB = 256 if g == 0 else 64
                for j in range(32):
                    pb = 2 + j % 2
                    for c in range(8):
                        k.mm(PS[:, pb * 512: pb * 512 + TB], W1[:, c, j * 128:(j + 1) * 128], h2T[par][:, c, 0:TB],
                             c == 0, c == 7, [r_w1[j // 4], r_h2T[par]], [RB[pb]])
                    k.act(rr[j % 2][:, 0:TB], PS[:, pb * 512: pb * 512 + TB], AF.Relu, [RB[pb]], [r_rr[j % 2]])
                    k.tt("pool", uT[:, j, 0:TB], rr[j % 2][:, 0:TB], rr[j % 2][:, 0:TB], ALU.mult, [r_rr[j % 2]], [r_uT])

            def ffn2B(g, bi, par):
                tok0 = bi * 256 if g == 0 else TP
                ysrc = y_p if g == 0 else y_s
                ntl = 2 if g == 0 else 1
                P = 128 if g == 0 else 64
                for tl in range(ntl):
                    xb_ = x1t[2 * par + tl]
                    rx = r_x1t[2 * par + tl]
                    for hh in range(2):
                        pb = 4 + (2 * tl + hh) % 4
                        for j in range(32):
                            k.mm(PS[:P, pb * 512:(pb + 1) * 512], uT[:, j, tl * 128: tl * 128 + P],
                                 W2[:, j, hh * 512:(hh + 1) * 512], j == 0, j == 31, [r_uT, r_w2[j // 4]], [RB[pb]])
                        cs = slice(hh * 512, (hh + 1) * 512)
                        k.tt("dve", tmpf[:P, cs], PS[:P, pb * 512:(pb + 1) * 512], G2B[:P, cs], ALU.mult,
                             [RB[pb], r_g2], [r_tmpf])
                        k.tt("pool", xb_[:P, cs], tmpf[:P, cs], xb_[:P, cs], ALU.add, [r_tmpf, rx], [rx])
                    k.act(tmpf[:P, :], xb_[:P, :], AF.Square, [rx], [r_tmpf, r_smb2], accum=smb2[:P, 4:5])
                    k.act(smb2[:P, 5:6], smb2[:P, 4:5], AF.Ln, [r_smb2, r_epsb], [r_smb2], scale=1.0 / D, bias=epsb[:P, :])
                    k.act(smb2[:P, 6:7], smb2[:P, 5:6], AF.Exp, [r_smb2], [r_smb2], scale=-0.5)
                    k.stt(tmpf[:P, :], xb_[:P, :], smb2[:P, 6:7], NFB[:P, :], ALU.mult, ALU.mult,
                          [rx, r_smb2, r_nf], [r_tmpf])
                    yo = tok0 - (0 if g == 0 else TP) + tl * 128
                    k.dma("sp", ysrc[yo:yo + P, :], tmpf[:P, :], [r_tmpf], [])

            if CFG["B"]:
                prepB(0)
                nb = CFG["nblk"]
                prologueB(0, 0, 0)
                for bi in range(nb):
                    par = bi % 2
                    ffn1B(0, bi, par)
                    if bi + 1 < nb:
                        prologueB(0, bi + 1, 1 - par)
                    ffn2B(0, bi, par)
                if CFG["sample"]:
                    prepB(1)
                    prologueB(1, 0, 0)
                    ffn1B(1, 0, 0)
                    ffn2B(1, 0, 0)
            S.flush()
    return nc


def _consts():
    ident = np.eye(128, dtype=np.float32)
    s = np.arange(128)[:, None]
    t = np.arange(128)[None, :]
    tri = ((s <= t) & (s // 64 == t // 64)).astype(np.float32)
    reset = np.ones((128, 512), np.float32)
    reset[:, 0::64] = 0.0
    p2 = -(2.0 ** -(np.arange(NIT + 1) + 1.0))
    pow2 = np.tile(p2[None, :], (128, 1)).astype(np.float32)
    half = 8
    inv = np.power(np.float32(500000.0), -np.arange(half, dtype=np.float32) * np.float32(2.0 / 16)).astype(np.float32)
    pos = np.zeros((17, 128), np.float32)
    for i in range(16):
        pos[i] = np.arange(128) + 128 * i
    pos[16, :64] = 2048 + np.arange(64)
    ang = pos[:, :, None].astype(np.float32) * inv[None, None, :]
    cos = np.cos(ang).astype(np.float32).transpose(1, 0, 2).copy()
    sin = np.sin(ang).astype(np.float32).transpose(1, 0, 2).copy()
    return {"c_ident": ident, "c_tri": tri, "c_reset": reset, "c_pow2": pow2, "c_cos": cos, "c_sin": sin}


_NC_CACHE = {}


def kernel(x_prompt, x_sample, cache_k, cache_v, cache_k_idx, state_hgrn, c_prompt, c_sample,
           w_mod, b_mod, norm1, w_in, lb_logits, g_norm_a, w_out, norm2, w_ff1, w_ff2, norm_f, _trace=False):
    f = lambda a: np.ascontiguousarray(np.asarray(a, dtype=np.float32))
    if "nc" not in _NC_CACHE:
        _NC_CACHE["nc"] = build_program()
    nc = _NC_CACHE["nc"]
    consts = _consts()
    shared = {"w_mod": f(w_mod)[0], "b_mod": f(b_mod)[0], "norm1": f(norm1)[0], "w_in": f(w_in)[0],
              "lb_logits": f(lb_logits), "g_norm_a": f(g_norm_a)[0], "w_out": f(w_out)[0], "norm2": f(norm2)[0],
              "w_ff1": f(w_ff1)[0], "w_ff2": f(w_ff2)[0], "norm_f": f(norm_f)}
    shared.update(consts)
    xp, xs = f(x_prompt), f(x_sample)
    ck, cv, cki, s0 = f(cache_k)[0], f(cache_v)[0], f(cache_k_idx)[0], f(state_hgrn)[0]
    cp_, cs_ = f(c_prompt), f(c_sample)
    in_maps = []
    for b in range(8):
        m = dict(shared)
        m["x_p"] = xp[b]
        m["x_s"] = xs[b]
        m["ck"] = ck[b].reshape(PAST, 256)
        m["cv"] = cv[b].reshape(PAST, 256)
        m["cki"] = cki[b]
        m["s0"] = s0[b]
        m["c2"] = np.stack([cp_[b], cs_[b]], 0)
        in_maps.append(m)
    res = run_bass_kernel_spmd(nc, in_maps, core_ids=list(range(8)), **({"trace": True} if _trace else {}))
    R = res.results
    st = lambda name: np.stack([np.asarray(R[b][name], dtype=np.float32) for b in range(8)], 0)
    outs = (st("y_p"), st("y_s"),
            st("k_p").reshape(1, 8, TP, 4, 64), st("v_p").reshape(1, 8, TP, 4, 64), st("ki_p").reshape(1, 8, TP, 64),
            st("S_p").reshape(1, 8, 4, 128, 128),
            st("k_s").reshape(1, 8, TS, 4, 64), st("v_s").reshape(1, 8, TS, 4, 64), st("ki_s").reshape(1, 8, TS, 64),
            st("S_s").reshape(1, 8, 4, 128, 128))
    if DEBUG:
        _NC_CACHE["dbg"] = {n: np.stack([np.asarray(R[b][n]) for b in range(8)], 0)
                            for n in ("dbg_x1", "dbg_mod", "dbg_z", "dbg_mix", "dbg_acc")}
        _NC_CACHE["res"] = res
    return outs
```

```python
from contextlib import ExitStack
import numpy as np
import concourse.bass as bass
import concourse.mybir as mybir
from concourse.bass_utils import run_bass_kernel_spmd

F32 = mybir.dt.float32
BF16 = mybir.dt.bfloat16
ALU = mybir.AluOpType
AF = mybir.ActivationFunctionType
AX = mybir.AxisListType

ENG = ["pe", "act", "dve", "pool", "sp"]
NPOOL = 12
DEBUG = False
CFG = {"A": True, "B": True, "ntiles": 16, "sample": True, "hgrn": True, "dsa": True, "wout": True, "nblk": 8, "scp": True}

D = 1024
TP = 2048
TS = 64
PAST = 2048
NTOK = TP + TS
NKEYMAX = 2176
NZ = 3720
Z_QA, Z_FA, Z_IA, Z_GA = 0, 512, 1024, 1536
Z_QB, Z_KB, Z_QI, Z_KI, Z_KI2, Z_VB, Z_WI = 2048, 2560, 2816, 3328, 3392, 3456, 3712
QB_PERM = [0, 2, 1, 3, 4, 6, 5, 7]
NROPE = 22
NIT = 15
EPS = 1e-6
NEG = -60000.0
ZBLOCKS = [(i * 512, min(512, NZ - i * 512)) for i in range(8)]


def w_in_segments():
    segs = [(0, 0, 512), (512, 512, 512), (1024, 1024, 512), (1536, 1536, 512)]
    for j, h in enumerate(QB_PERM):
        segs.append((Z_QB + 64 * j, 2048 + 64 * h, 64))
    segs.append((Z_KB, 2560, 256))
    segs.append((Z_QI, 3072, 512))
    segs.append((Z_KI, 3584, 64))
    segs.append((Z_KI2, 3584, 64))
    segs.append((Z_VB, 2816, 256))
    segs.append((Z_WI, 3648, 8))
    return segs


class Res:
    __slots__ = ("name", "w", "r", "excl")

    def __init__(self, name="", excl=False):
        self.name = name
        self.w = None
        self.r = {}
        self.excl = excl


class Sched:
    def __init__(self, nc, stack):
        self.nc = nc
        self.ops = {e: [] for e in ENG}
        self.tick = {e: 0 for e in ENG}
        self.sem = {e: stack.enter_context(nc.semaphore("s_" + e)) for e in ENG}
        self.known = {e: {} for e in ENG}
        self.dq = ("sp", "pool", "act")
        self.dsem = {q: [stack.enter_context(nc.semaphore("d_%s%d" % (q, i))) for i in range(NPOOL)]
                     for q in self.dq}
        self.dcnt = {q: 0 for q in self.dq}
        self.duse = {q: [0] * NPOOL for q in self.dq}
        self.all_dma_ev = {}
        self.same_eng = {"pool", "dve", "act"}

    def _semof(self, k):
        if k[0] == "e":
            return self.sem[k[1]]
        return self.dsem[k[1]][k[2]]

    def _collect(self, eng, reads, writes, eng_is_dma=False):
        waits = {}
        known = self.known[eng]

        def need(ev, war=False):
            if ev is None:
                return
            k, v = ev
            if k == ("e", eng):
                if eng not in self.same_eng:
                    return
            if known.get(k, 0) >= v:
                return
            if waits.get(k, 0) < v:
                waits[k] = v

        is_dma = eng_is_dma
        for r in reads:
            if r.w:
                for k, v in r.w.items():
                    need((k, v))
        for w in writes:
            if w.w:
                for k, v in w.w.items():
                    if is_dma and k[0] == "d":
                        continue
                    need((k, v))
            for k, v in w.r.items():
                need((k, v), war=True)
        return waits

    def _commit(self, eng, waits, ev, reads, writes):
        for k, v in waits.items():
            self.known[eng][k] = v
        k, v = ev
        for r in reads:
            if r.r.get(k, 0) < v:
                r.r[k] = v
        for w in writes:
            if ev[0][0] == "d" and w.w and all(kk[0] == "d" for kk in w.w):
                w.w[ev[0]] = ev[1]
            else:
                w.w = {ev[0]: ev[1]}
            w.r = {}

    def op(self, eng, fn, reads=(), writes=()):
        ex = [r for r in reads if r.excl]
        if ex:
            writes = list(writes) + [r for r in ex if r not in writes]
            reads = [r for r in reads if not r.excl]
        waits = self._collect(eng, reads, writes)
        self.tick[eng] += 1
        ev = (("e", eng), self.tick[eng])
        self.ops[eng].append((list(waits.items()), fn, ("e", eng)))
        self._commit(eng, waits, ev, reads, writes)
        return ev

    def dma(self, q, fn, reads=(), writes=()):
        i = self.dcnt[q] % NPOOL
        self.dcnt[q] += 1
        prev = self.duse[q][i]
        waits = self._collect(q, reads, writes, eng_is_dma=True)
        k = ("d", q, i)
        if prev > 0 and self.known[q].get(k, 0) < 16 * prev:
            waits[k] = 16 * prev
        self.duse[q][i] = prev + 1
        ev = (k, 16 * (prev + 1))
        self.ops[q].append((list(waits.items()), fn, k))
        self._commit(q, waits, ev, reads, writes)
        self.all_dma_ev[k] = 16 * (prev + 1)
        return ev

    def flush(self):
        waits = []
        for k, v in self.all_dma_ev.items():
            if self.known["sp"].get(k, 0) < v:
                waits.append((k, v))
        self.ops["sp"].append((waits, None, None))
        sched = self

        def run(name, e):
            for w, fn, inc in sched.ops[name]:
                for k, v in w:
                    e.wait_ge(sched._semof(k), v)
                if fn is None:
                    continue
                ins = fn(e)
                if inc[0] == "e":
                    ins.then_inc(sched.sem[name], 1)
                else:
                    ins.then_inc(sched._semof(inc), 16)

        with self.nc.Block() as block:
            @block.tensor
            def _(e):
                run("pe", e)

            @block.scalar
            def _(e):
                run("act", e)

            @block.vector
            def _(e):
                run("dve", e)

            @block.gpsimd
            def _(e):
                run("pool", e)

            @block.sync
            def _(e):
                run("sp", e)

        for e in ENG:
            self.ops[e] = []
            for e2 in ENG:
                self.known[e][("e", e2)] = self.tick[e2]
            for k, v in self.all_dma_ev.items():
                self.known[e][k] = v


class K:
    def __init__(self, nc, S):
        self.nc = nc
        self.S = S

    def mm(self, out, lhsT, rhs, start, stop, reads, writes):
        self.S.op("pe", lambda e: e.matmul(out, lhsT=lhsT, rhs=rhs, start=start, stop=stop,
                                           skip_group_check=True), reads, writes)

    def tr(self, out, in_, ident, reads, writes):
        self.S.op("pe", lambda e: e.transpose(out=out, in_=in_, identity=ident), reads, writes)

    def act(self, out, in_, func, reads, writes, scale=None, bias=None, accum=None):
        kw = {}
        if scale is not None:
            kw["scale"] = scale
        if bias is not None:
            kw["bias"] = bias
        if accum is not None:
            kw["accum_out"] = accum
        self.S.op("act", lambda e: e.activation(out=out, in_=in_, func=func, **kw), reads, writes)

    def tt(self, eng, out, in0, in1, op, reads, writes):
        self.S.op(eng, lambda e: e.tensor_tensor(out=out, in0=in0, in1=in1, op=op), reads, writes)

    def ts(self, eng, out, in0, s1, s2, op0, op1, reads, writes, accum=None):
        if op1 is None:
            self.S.op(eng, lambda e: e.tensor_scalar(out=out, in0=in0, scalar1=s1, scalar2=None, op0=op0),
                      reads, writes)
        elif accum is not None:
            self.S.op(eng, lambda e: e.tensor_scalar(out=out, in0=in0, scalar1=s1, scalar2=s2, op0=op0, op1=op1,
                                                     accum_out=accum), reads, writes)
        else:
            self.S.op(eng, lambda e: e.tensor_scalar(out=out, in0=in0, scalar1=s1, scalar2=s2, op0=op0, op1=op1),
                      reads, writes)

    def stt(self, out, in0, scalar, in1, op0, op1, reads, writes):
        self.S.op("dve", lambda e: e.scalar_tensor_tensor(out=out, in0=in0, scalar=scalar, in1=in1, op0=op0, op1=op1),
                  reads, writes)

    def cp(self, eng, out, in_, reads, writes):
        if eng == "act":
            self.act(out, in_, AF.Copy, reads, writes)
        else:
            self.S.op(eng, lambda e: e.tensor_copy(out=out, in_=in_), reads, writes)

    def memset(self, eng, ap, val, writes):
        self.S.op(eng, lambda e: e.memset(ap, val), (), writes)

    def recip(self, out, in_, reads, writes):
        self.S.op("dve", lambda e: e.reciprocal(out=out, in_=in_), reads, writes)

    def dma(self, q, out, in_, reads, writes, slow=False):
        if slow:
            self.S.dma(q, lambda e: e.dma_start(out=out, in_=in_, allow_slow_non_contiguous=True), reads, writes)
        else:
            self.S.dma(q, lambda e: e.dma_start(out=out, in_=in_), reads, writes)


def build_program():
    nc = bass.Bass("TRN2", target_bir_lowering=False)

    def din(name, shape, dt=F32):
        return nc.dram_tensor(name, list(shape), dt, kind="ExternalInput").ap()

    def dout(name, shape, dt=F32):
        return nc.dram_tensor(name, list(shape), dt, kind="ExternalOutput").ap()

    x_p = din("x_p", [TP, D])
    x_s = din("x_s", [TS, D])
    ck_d = din("ck", [PAST, 256])
    cv_d = din("cv", [PAST, 256])
    cki_d = din("cki", [PAST, 64])
    s0_d = din("s0", [4, 128, 128])
    c2_d = din("c2", [2, D])
    w_mod_d = din("w_mod", [D, 6 * D])
    b_mod_d = din("b_mod", [6 * D])
    norm1_d = din("norm1", [D])
    w_in_d = din("w_in", [D, 3656])
    lbl_d = din("lb_logits", [2, 512])
    gna_d = din("g_norm_a", [512])
    w_out_d = din("w_out", [D, D])
    norm2_d = din("norm2", [D])
    w_ff1_d = din("w_ff1", [D, 4 * D])
    w_ff2_d = din("w_ff2", [4 * D, D])
    normf_d = din("norm_f", [D])
    ident_d = din("c_ident", [128, 128])
    tri_d = din("c_tri", [128, 128])
    reset_d = din("c_reset", [128, 512])
    pow2_d = din("c_pow2", [128, NIT + 1])
    cos_d = din("c_cos", [128, 17, 8])
    sin_d = din("c_sin", [128, 17, 8])

    y_p = dout("y_p", [TP, D])
    y_s = dout("y_s", [TS, D])
    k_p = dout("k_p", [TP, 256])
    v_p = dout("v_p", [TP, 256])
    ki_p = dout("ki_p", [TP, 64])
    S_p = dout("S_p", [4, 128, 128])
    k_s = dout("k_s", [TS, 256])
    v_s = dout("v_s", [TS, 256])
    ki_s = dout("ki_s", [TS, 64])
    S_s = dout("S_s", [4, 128, 128])

    modS = nc.dram_tensor("modS", [2, 6 * D], F32, kind="Internal").ap()
    x1S = nc.dram_tensor("x1S", [NTOK, D], F32, kind="Internal").ap()
    r_modS = Res()
    r_x1S = [Res() for _ in range(17)]
    if DEBUG:
        dbg_x1 = dout("dbg_x1", [NTOK, D])
        dbg_mod = dout("dbg_mod", [2, 6 * D])
        dbg_z = dout("dbg_z", [128, NZ])
        dbg_mix = dout("dbg_mix", [128, 8, 128], BF16)
        dbg_acc = dout("dbg_acc", [128, NKEYMAX])

    with ExitStack() as st:
        S = Sched(nc, st)
        k = K(nc, S)

        def sbt(stack, name, shape, dt):
            return stack.enter_context(nc.sbuf_tensor(name, list(shape), dt))

        PS = st.enter_context(nc.psum_tensor("ps", [128, 4096], F32))
        RB = [Res("bank%d" % i, excl=True) for i in range(8)]

        def bank(b, ncols=512, off=0):
            return PS[:, b * 512 + off: b * 512 + off + ncols]

        identf = sbt(st, "identf", [128, 128], F32); r_identf = Res()
        identb = sbt(st, "identb", [128, 128], BF16); r_identb = Res()
        onesb = sbt(st, "onesb", [128, 128], BF16); r_onesb = Res()
        zb512 = sbt(st, "zb512", [128, 512], BF16); r_zb = Res()

        k.dma("sp", identf[:], ident_d, [], [r_identf])
        k.cp("dve", identb[:], identf[:], [r_identf], [r_identb])
        k.memset("dve", onesb[:], 1.0, [r_onesb])
        k.memset("dve", zb512[:], 0.0, [r_zb])

        with ExitStack() as p0:
            cT = sbt(p0, "cT", [128, 2, 8], F32); r_cT = Res()
            cE = sbt(p0, "cE", [128, 2, 8], F32); r_cE = Res()
            scT = sbt(p0, "scT", [128, 8, 2], BF16); r_scT = Res()
            wm = [sbt(p0, "wm%d" % i, [128, 8, 1024], BF16) for i in range(2)]
            r_wm = [Res(), Res()]
            bmod2 = sbt(p0, "bmod2", [2, 6 * D], F32); r_bmod2 = Res()
            modrow = sbt(p0, "modrow", [2, 6 * D], F32); r_modrow = Res()

            for g_ in range(2):
                k.dma("sp", cT[:, g_, :], c2_d[g_, :].rearrange("(c p) -> p c", p=128), [], [r_cT], slow=True)
            k.dma("sp", bmod2[:], b_mod_d.partition_broadcast(2), [], [r_bmod2])
            k.act(cE[:], cT[:], AF.Exp, [r_cT], [r_cE], scale=-1.0)
            k.ts("dve", cE[:], cE[:], 1.0, None, ALU.add, None, [r_cE], [r_cE])
            k.recip(cE[:], cE[:], [r_cE], [r_cE])
            k.tt("dve", scT[:].rearrange("p c g -> p g c"), cT[:], cE[:], ALU.mult, [r_cT, r_cE], [r_scT])
            wmv = w_mod_d.rearrange("(c p) n -> p c n", p=128)
            for m in range(6):
                b = m % 2
                for hh in range(2):
                    k.dma("pool", wm[b][:, :, hh * 512:(hh + 1) * 512],
                          wmv[:, :, m * 1024 + hh * 512: m * 1024 + (hh + 1) * 512], [], [r_wm[b]])
                for hh in range(2):
                    pb = (2 * m + hh) % 2
                    for c in range(8):
                        k.mm(PS[0:2, pb * 512: pb * 512 + 512], scT[:, c, :], wm[b][:, c, hh * 512:(hh + 1) * 512],
                             c == 0, c == 7, [r_scT, r_wm[b]], [RB[pb]])
                    cs = slice(m * 1024 + hh * 512, m * 1024 + (hh + 1) * 512)
                    k.tt("dve", modrow[:, cs], PS[0:2, pb * 512: pb * 512 + 512], bmod2[:, cs], ALU.add,
                         [RB[pb], r_bmod2], [r_modrow])
            k.dma("sp", modS, modrow[:], [r_modrow], [r_modS])
            if DEBUG:
                k.dma("sp", dbg_mod, modrow[:], [r_modrow], [])
            S.flush()

        with ExitStack() as pa:
            W_IN = sbt(pa, "W_IN", [128, 8, NZ], BF16); r_win = Res()
            W_OUT = sbt(pa, "W_OUT", [128, 8, D], BF16); r_wout = Res()
            zbiasB = sbt(pa, "zbiasB", [128, NZ], F32); r_zbias = Res()
            G1B = sbt(pa, "G1B", [128, D], F32); r_g1b = Res()
            n1c = sbt(pa, "n1c", [128, 8], F32); r_n1c = Res()
            sc1c = sbt(pa, "sc1c", [128, 8], F32); r_sc1c = Res()
            sh1c = sbt(pa, "sh1c", [128, 8], F32); r_sh1c = Res()
            gam1c = sbt(pa, "gam1c", [128, 8], F32); r_gam1c = Res()
            sh1bc = sbt(pa, "sh1bc", [128, 8, 128], BF16); r_sh1bc = Res()
            xt = [sbt(pa, "xt%d" % i, [128, D], F32) for i in range(2)]; r_xt = [Res(), Res()]
            smz = sbt(pa, "smz", [128, 4], F32); r_smz = Res()
            wtmp = sbt(pa, "wtmp", [128, 512], F32); r_wtmp = Res()
            hT = sbt(pa, "hT", [128, 8, 128], BF16); r_hT = Res()
            z = sbt(pa, "z", [128, NZ], F32); r_zA = Res(); r_zB = Res()
            rtmp = sbt(pa, "rtmp", [128, 4, NROPE * 8], F32); r_rtmp = Res()
            KT = sbt(pa, "KT", [128, 2, NKEYMAX], BF16); r_KT = Res()
            V1 = sbt(pa, "V1", [128, 17, 4, 65], BF16); r_V1 = Res()
            kIT = sbt(pa, "kIT", [128, NKEYMAX], BF16); r_kIT = Res()
            qz = sbt(pa, "qz", [128, 4, 2, 128], BF16); r_qz = Res()
            qIz = sbt(pa, "qIz", [128, 8, 128], BF16); r_qIz = Res()
            ONE_T = sbt(pa, "one_t", [128, 1], F32); r_one = Res()
            wI = sbt(pa, "wI", [128, 8], F32); r_wI = Res()
            nhwh = sbt(pa, "nhwh", [128, NIT + 1], F32)
            kbc = sbt(pa, "kbc", [128, 1], F32); r_kbc = Res()
            acc = sbt(pa, "acc", [128, NKEYMAX], F32); r_accb = [Res() for _ in range(5)]
            nmask = sbt(pa, "nmask", [128, NKEYMAX], BF16); r_nmask = Res()
            nmT = sbt(pa, "nmT", [128, 17, 128], BF16); r_nmT = Res()
            pT2 = [sbt(pa, "pT%d" % i, [128, 512], BF16) for i in range(2)]
            r_pT = [Res() for _ in range(2)]
            mixB = sbt(pa, "mixB", [128, 512], BF16); r_mixB = Res()
            mixTs = [sbt(pa, "mixT%d" % i, [128, 8, 128], BF16) for i in range(2)]; r_mixTs = [Res(), Res()]
            sm = sbt(pa, "sm", [128, 64], F32); r_sm = Res()
            hwtab = sbt(pa, "hwtab", [128, NIT + 1], F32); r_hwtab = Res()
            pow2 = sbt(pa, "pow2", [128, NIT + 1], F32); r_pow2 = Res()
            cosT = sbt(pa, "cosT", [128, 17, 8], F32); r_cos = Res()
            sinT = sbt(pa, "sinT", [128, 17, 8], F32)
            tri = sbt(pa, "tri", [128, 128], F32); r_tri = Res()
            resetm = sbt(pa, "resetm", [128, 512], F32); r_reset = Res()
            lbl = sbt(pa, "lbl", [128, 2, 4], F32); r_lbl = Res()
            lbc = sbt(pa, "lbc", [128, 4], F32); r_lbc = Res()
            omlb = sbt(pa, "omlb", [128, 4], F32)
            nomlb = sbt(pa, "nomlb", [128, 4], F32)
            gnc = sbt(pa, "gnc", [128, 4], F32); r_gnc = Res()
            Sf = sbt(pa, "Sf", [128, 4, 128], F32); r_Sf = Res()
            Sb = [sbt(pa, "Sb%d" % i, [128, 4, 128], BF16) for i in range(2)]
            r_Sb = [Res(), Res()]
            hq = sbt(pa, "hq", [128, 512], F32)
            hg = sbt(pa, "hg", [128, 512], F32)
            hlf = sbt(pa, "hlf", [128, 512], F32)
            hk = sbt(pa, "hk", [128, 512], F32)
            hb = sbt(pa, "hb", [128, 512], F32)
            heb = sbt(pa, "heb", [128, 512], F32)
            r_hg = Res(); r_hq = Res(); r_hlf = Res(); r_hk = Res(); r_hb = Res(); r_heb = Res()
            henb = hlf; r_henb = r_hlf
            qtl = sbt(pa, "qtl", [128, 512], BF16); r_qtl = Res()
            ktl = sbt(pa, "ktl", [128, 512], BF16); r_ktl = Res()
            kdT = sbt(pa, "kdT", [128, 512], BF16); r_kdT = Res()
            kdz = [sbt(pa, "kdz%d" % i, [128, 4, 128], BF16) for i in range(2)]
            r_kdz = [Res(), Res()]
            vA = sbt(pa, "vA", [128, 512], BF16); r_vA = Res()
            atm = sbt(pa, "atm", [128, 512], BF16); r_atm = Res()
            sqb = sbt(pa, "sqb", [128, 512], BF16); r_sqb = Res()
            cst = rtmp[:].rearrange("p a b -> p (a b)")[:, 0:640]; r_cst = r_rtmp

            k.dma("sp", tri[:], tri_d, [], [r_tri])
            k.dma("sp", resetm[:], reset_d, [], [r_reset])
            k.dma("sp", pow2[:], pow2_d, [], [r_pow2])
            k.dma("sp", cosT[:], cos_d, [], [r_cos])
            k.dma("sp", sinT[:], sin_d, [], [r_cos])
            k.dma("sp", lbl[:], lbl_d.rearrange("r (h p) -> p r h", p=128), [], [r_lbl], slow=True)
            k.dma("sp", gnc[:], gna_d.rearrange("(h p) -> p h", p=128), [], [r_gnc], slow=True)
            k.dma("sp", n1c[:], norm1_d.rearrange("(c p) -> p c", p=128), [], [r_n1c], slow=True)
            wiv = w_in_d.rearrange("(c p) n -> p c n", p=128)
            for (mc, oc, ln) in w_in_segments():
                k.dma("pool", W_IN[:, :, mc:mc + ln], wiv[:, :, oc:oc + ln], [], [r_win])
            wov = w_out_d.rearrange("(c p) n -> p c n", p=128)
            for hh in range(2):
                k.dma("pool", W_OUT[:, :, hh * 512:(hh + 1) * 512], wov[:, :, hh * 512:(hh + 1) * 512], [], [r_wout])
            k.tt("dve", lbc[:], lbl[:, 0, :], lbl[:, 1, :], ALU.subtract, [r_lbl], [r_lbc])
            k.act(lbc[:], lbc[:], AF.Exp, [r_lbc], [r_lbc], scale=-1.0)
            k.ts("dve", lbc[:], lbc[:], 1.0, None, ALU.add, None, [r_lbc], [r_lbc])
            k.recip(lbc[:], lbc[:], [r_lbc], [r_lbc])
            k.ts("dve", omlb[:], lbc[:], -1.0, 1.0, ALU.mult, ALU.add, [r_lbc], [r_lbc])
            k.ts("dve", nomlb[:], omlb[:], -1.0, None, ALU.mult, None, [r_lbc], [r_lbc])
            k.memset("pool", qz[:], 0.0, [r_qz])
            k.memset("pool", qIz[:], 0.0, [r_qIz])
            k.memset("dve", ONE_T[:], 1.0, [r_one])
            k.memset("pool", kdz[0][:], 0.0, [r_kdz[0]])
            k.memset("pool", kdz[1][:], 0.0, [r_kdz[1]])
            k.memset("pool", V1[:], 1.0, [r_V1])
            k.memset("dve", Sf[:], 0.0, [r_Sf])
            k.memset("dve", Sb[0][:], 0.0, [r_Sb[0]])

            def prep_group(g):
                k.dma("sp", sh1c[:], modS[g, 0:D].rearrange("(c p) -> p c", p=128), [r_modS], [r_sh1c], slow=True)
                k.dma("sp", sc1c[:], modS[g, D:2 * D].rearrange("(c p) -> p c", p=128), [r_modS], [r_sc1c], slow=True)
                k.dma("sp", G1B[:], modS[g, 2 * D:3 * D].partition_broadcast(128), [r_modS], [r_g1b])
                k.ts("dve", gam1c[:], sc1c[:], 1.0, None, ALU.add, None, [r_sc1c], [r_gam1c])
                k.tt("dve", gam1c[:], gam1c[:], n1c[:], ALU.mult, [r_gam1c, r_n1c], [r_gam1c])
                k.cp("dve", sh1bc[:], sh1c[:].unsqueeze(2).broadcast_to([128, 8, 128]), [r_sh1c], [r_sh1bc])
                for bi, (c0, ncol) in enumerate(ZBLOCKS):
                    pb = 2 + bi % 2
                    for c in range(8):
                        k.mm(bank(pb, ncol), sh1bc[:, c, :], W_IN[:, c, c0:c0 + ncol], c == 0, c == 7,
                             [r_sh1bc, r_win], [RB[pb]])
                    k.cp("act", zbiasB[:, c0:c0 + ncol], bank(pb, ncol), [RB[pb]], [r_zbias])

            def zstage(g, ti):
                P = 128 if g == 0 else 64
                t0 = ti * 128 if g == 0 else 0
                xsrc = x_p if g == 0 else x_s
                tix = ti if g == 0 else 16
                xb_ = xt[tix % 2]
                rx = r_xt[tix % 2]
                k.dma("sp", xb_[:P, :], xsrc[t0:t0 + P, :], [], [rx])
                k.act(wtmp[:P, :], xb_[:P, 0:512], AF.Square, [rx], [r_wtmp, r_smz], accum=smz[:P, 0:1])
                k.act(wtmp[:P, :], xb_[:P, 512:1024], AF.Square, [rx], [r_wtmp, r_smz], accum=smz[:P, 3:4])
                k.ts("dve", smz[:P, 3:4], smz[:P, 3:4], 1.0 / D, EPS, ALU.mult, ALU.add, [r_smz], [r_smz])
                k.act(smz[:P, 1:2], smz[:P, 0:1], AF.Ln, [r_smz], [r_smz], scale=1.0 / D, bias=smz[:P, 3:4])
                k.act(smz[:P, 2:3], smz[:P, 1:2], AF.Exp, [r_smz], [r_smz], scale=-0.5)
                rstd = smz[:P, 2:3]
                yield
                xTp = PS[:, 0:1024].rearrange("p (c t) -> p c t", c=8)
                for c in range(8):
                    k.tr(xTp[:, c, :P], xb_[:P, c * 128:(c + 1) * 128], identf[:P, :P], [rx, r_identf],
                         [RB[0] if c < 4 else RB[1]])
                yield
                k.tt("dve", hT[:, :, :P], xTp[:, :, :P], gam1c[:, :].unsqueeze(2).broadcast_to([128, 8, P]), ALU.mult,
                     [RB[0], RB[1], r_gam1c], [r_hT])
                yield
                for bi, (c0, ncol) in enumerate(ZBLOCKS):
                    pb = 2 + bi % 2
                    for c in range(8):
                        k.mm(PS[:P, pb * 512: pb * 512 + ncol], hT[:, c, :P], W_IN[:, c, c0:c0 + ncol], c == 0, c == 7,
                             [r_hT, r_win], [RB[pb]])
                    k.stt(z[:P, c0:c0 + ncol], PS[:P, pb * 512: pb * 512 + ncol], rstd, zbiasB[:P, c0:c0 + ncol],
                          ALU.mult, ALU.add, [RB[pb], r_smz, r_zbias], [r_zA if c0 < 2048 else r_zB])
                    yield
                zr = z[:P, Z_QB:Z_QB + NROPE * 64].rearrange("p (h d) -> p h d", d=64)
                x1v = zr[:, :, 0:8]
                x2v = zr[:, :, 8:16]
                cb_ = cosT[:P, tix, :].unsqueeze(1).broadcast_to([P, NROPE, 8])
                sb_ = sinT[:P, tix, :].unsqueeze(1).broadcast_to([P, NROPE, 8])
                tv = [rtmp[:P, j, :].rearrange("p (h d) -> p h d", d=8) for j in range(4)]
                k.tt("pool", tv[0], x1v, cb_, ALU.mult, [r_zB, r_cos], [r_rtmp])
                k.tt("pool", tv[1], x2v, sb_, ALU.mult, [r_zB, r_cos], [r_rtmp])
                k.tt("pool", tv[2], x2v, cb_, ALU.mult, [r_zB, r_cos], [r_rtmp])
                k.tt("pool", tv[3], x1v, sb_, ALU.mult, [r_zB, r_cos], [r_rtmp])
                k.tt("pool", x1v, tv[0], tv[1], ALU.subtract, [r_rtmp], [r_zB])
                k.tt("pool", x2v, tv[2], tv[3], ALU.add, [r_rtmp], [r_zB])
                yield
                ko, vo, kio = (k_p, v_p, ki_p) if g == 0 else (k_s, v_s, ki_s)
                k.dma("sp", ko[t0:t0 + P, :], z[:P, Z_KB:Z_KB + 256], [r_zB], [])
                k.dma("sp", vo[t0:t0 + P, :], z[:P, Z_VB:Z_VB + 256], [r_zB], [])
                k.dma("sp", kio[t0:t0 + P, :], z[:P, Z_KI:Z_KI + 64], [r_zB], [])
                if DEBUG and g == 0 and ti == 2:
                    k.dma("sp", dbg_z, z[:, :], [r_zA, r_zB], [])

            def drain(gen):
                for _ in gen:
                    pass

            def tile_params(g, ti):
                P = 128 if g == 0 else 64
                tix = ti if g == 0 else 16
                v3 = lambda t: t[:, 0:4 * P].rearrange("p (h t) -> p h t", h=4)
                return P, tix, v3

            def hgrn_of(g, ti):
                P, tix, v3 = tile_params(g, ti)
                return hgrn(g, ti, P, v3, mixTs[tix % 2], r_mixTs[tix % 2])

            def tileA(g, ti, nxt=None, hg_cur=None, hg_nxt=None):
                P, tix, v3 = tile_params(g, ti)
                t0 = ti * 128 if g == 0 else 0
                key0 = tix * 128
                nk = 128 * (ti + 1) if g == 0 else NKEYMAX
                nkb = nk // 128
                xb_ = xt[tix % 2]
                rx = r_xt[tix % 2]
                mixT = mixTs[tix % 2]
                r_mixT = r_mixTs[tix % 2]
                gd = dsa(g, ti, P, tix, key0, nk, nkb, mixT, r_mixT)

                def rr(gens, stop_mark):
                    hit = False
                    while gens and not hit:
                        for gen in list(gens):
                            try:
                                if next(gen) == stop_mark:
                                    hit = True
                            except StopIteration:
                                gens.remove(gen)

                rr([gd] + ([hg_cur] if hg_cur is not None else []), "BISECT")
                if hg_cur is not None:
                    drain(hg_cur)
                rr([gd] + ([nxt] if nxt is not None else []), "BISECT_END")
                if nxt is not None:
                    drain(nxt)
                rr([gd] + ([hg_nxt] if hg_nxt is not None else []), "NEVER")
                if DEBUG and g == 0 and ti == 2:
                    k.dma("sp", dbg_mix, mixT[:], [r_mixT], [])
                for hh in range(2):
                    pb = 4 + hh
                    for c in range(8):
                        k.mm(PS[:P, pb * 512:(pb + 1) * 512], mixT[:, c, :P], W_OUT[:, c, hh * 512:(hh + 1) * 512],
                             c == 0, c == 7, [r_mixT, r_wout], [RB[pb]])
                    k.tt("dve", wtmp[:P, :], PS[:P, pb * 512:(pb + 1) * 512],
                         G1B[:P, hh * 512:(hh + 1) * 512], ALU.mult, [RB[pb], r_g1b], [r_wtmp])
                    k.tt("dve", xb_[:P, hh * 512:(hh + 1) * 512], wtmp[:P, :],
                         xb_[:P, hh * 512:(hh + 1) * 512], ALU.add, [r_wtmp, rx], [rx])
                tok0 = t0 if g == 0 else TP
                k.dma("sp", x1S[tok0:tok0 + P, :], xb_[:P, :], [rx], [r_x1S[tix]])

            def hgrn(g, ti, P, v3, mixT, r_mixT):
                N4 = 4 * P
                hp = PS[:, 2048:2048 + 12 * 128].rearrange("p (j t) -> p j t", j=12)
                for j in range(12):
                    grp = j // 4
                    c0 = (Z_QA, Z_FA, Z_GA)[grp] + (j % 4) * 128
                    k.tr(hp[:, j, :P], z[:P, c0:c0 + 128], identf[:P, :P], [r_zA, r_identf], [RB[4 + grp]])
                yield
                k.cp("pool", vA[:P, :], z[:P, Z_IA:Z_IA + 512], [r_zA], [r_vA])
                Eqf = z[:, 0:8 * P].rearrange("p (j t) -> p j t", j=8)
                Eg = z[:, 1536:1536 + 4 * P].rearrange("p (j t) -> p j t", j=4)
                k.act(Eqf[:, 0:4, :], hp[:, 0:4, :P], AF.Exp, [RB[4]], [r_zA], scale=-1.0)
                k.act(Eqf[:, 4:8, :], hp[:, 4:8, :P], AF.Exp, [RB[5]], [r_zA], scale=-1.0)
                k.act(Eg, hp[:, 8:12, :P], AF.Exp, [RB[6]], [r_zA], scale=-1.0)
                yield
                ff = z[:, 4 * P:8 * P]
                k.ts("dve", ff, ff, 1.0, None, ALU.add, None, [r_zA], [r_zA])
                k.recip(ff, ff, [r_zA], [r_zA])
                yield
                for fl in (z[:, 0:4 * P], z[:, 1536:1536 + 4 * P]):
                    k.act(fl, fl, AF.Ln, [r_zA, r_one], [r_zA], bias=ONE_T[:, :])
                    k.act(fl, fl, AF.Exp, [r_zA], [r_zA], scale=-1.0)
                    yield
                sigq, sigf, sigg = Eqf[:, 0:4, :], Eqf[:, 4:8, :], Eg
                k.tt("dve", v3(hq), hp[:, 0:4, :P], sigq, ALU.mult, [RB[4], r_zA], [r_hq])
                yield
                for h in range(4):
                    k.stt(v3(hg)[:, h, :], hp[:, 8 + h, :P], gnc[:, h:h + 1], sigg[:, h, :], ALU.mult, ALU.mult,
                          [RB[6], r_zA, r_gnc], [r_hg])
                yield
                for h in range(4):
                    k.act(v3(hlf)[:, h, :], sigf[:, h, :], AF.Ln, [r_zA, r_lbc], [r_hlf],
                          scale=omlb[:, h:h + 1], bias=lbc[:, h:h + 1])
                for h in range(4):
                    k.ts("dve", v3(hk)[:, h, :], sigf[:, h, :], nomlb[:, h:h + 1], omlb[:, h:h + 1], ALU.mult, ALU.add,
                         [r_zA, r_lbc], [r_hk])
                yield
                S.op("dve", lambda e: e.tensor_tensor_scan(out=hb[:, 0:N4], data0=resetm[:, 0:N4], data1=hlf[:, 0:N4],
                                                            initial=0.0, op0=ALU.mult, op1=ALU.add),
                     [r_hlf, r_reset], [r_hb])
                k.act(heb[:, 0:N4], hb[:, 0:N4], AF.Exp, [r_hb], [r_heb])
                k.act(henb[:, 0:N4], hb[:, 0:N4], AF.Exp, [r_hb], [r_henb], scale=-1.0)
                yield
                k.tt("dve", qtl[:, 0:N4], hq[:, 0:N4], heb[:, 0:N4], ALU.mult, [r_hq, r_heb], [r_qtl])
                k.tt("dve", ktl[:, 0:N4], hk[:, 0:N4], henb[:, 0:N4], ALU.mult, [r_hk, r_henb], [r_ktl])
                yield
                nck = P // 64
                for h in range(4):
                    for ck in range(nck):
                        le = ck * 64 + 63
                        k.act(v3(henb)[:, h, ck * 64:ck * 64 + 64], v3(hb)[:, h, ck * 64:ck * 64 + 64], AF.Exp,
                              [r_hb], [r_henb], scale=-1.0, bias=v3(hb)[:, h, le:le + 1])
                ab = PS[:, 7 * 512: 7 * 512 + 512]
                abv = ab[:, 0:N4].rearrange("p (h t) -> p h t", h=4)
                for h in range(4):
                    k.mm(abv[:P, h, :], v3(ktl)[:, h, :], v3(qtl)[:, h, :], True, True, [r_ktl, r_qtl], [RB[7]])
                yield
                k.tt("dve", kdT[:, 0:N4], hk[:, 0:N4], henb[:, 0:N4], ALU.mult, [r_hk, r_henb], [r_kdT])
                k.tt("dve", v3(atm)[:P], abv[:P], tri[:P, :P].unsqueeze(1).broadcast_to([P, 4, P]), ALU.mult,
                     [RB[7], r_tri], [r_atm])
                yield
                kp = PS[:, 7 * 512: 7 * 512 + 256].bitcast(BF16).rearrange("p (h d) -> p h d", h=4)
                for h in range(4):
                    k.tr(kp[:P, h, :], v3(kdT)[:, h, :], identb[:, :], [r_kdT, r_identb], [RB[7]])
                for ck in range(nck):
                    k.cp("act", kdz[ck][ck * 64:ck * 64 + 64, :, :], kp[ck * 64:ck * 64 + 64, :, :], [RB[7]], [r_kdz[ck]])
                yield
                ob = PS[:, 4 * 512: 4 * 512 + 512]
                obv = ob[:, 0:N4].rearrange("p (h t) -> p h t", h=4)
                k.mm(ob[:, 0:512], zb512[:, 0:128], zb512[:, 0:512], True, False, [r_zb], [RB[4]])
                for ck in range(nck):
                    sub = PS[:, (5 + ck) * 512: (5 + ck) * 512 + 512].rearrange("p (h d) -> p h d", h=4)
                    for h in range(4):
                        k.mm(sub[:, h, :], kdz[ck][:P, h, :], vA[:P, h * 128:(h + 1) * 128], True, True,
                             [r_kdz[ck], r_vA], [RB[5 + ck]])
                for h in range(4):
                    k.mm(obv[:, h, :], vA[:P, h * 128:(h + 1) * 128], v3(atm)[:P, h, :], False, False,
                         [r_vA, r_atm], [RB[4]])
                yield
                cur = hgrn_state["cur"]
                for ck in range(nck):
                    le = ck * 64 + 63
                    sub = PS[:, (5 + ck) * 512: (5 + ck) * 512 + 512].rearrange("p (h d) -> p h d", h=4)
                    for h in range(4):
                        k.mm(obv[:, h, ck * 64:ck * 64 + 64], Sb[cur][:, h, :], v3(qtl)[:, h, ck * 64:ck * 64 + 64],
                             False, False, [r_Sb[cur], r_qtl], [RB[4]])
                    nxt = 1 - cur
                    for h in range(4):
                        k.stt(Sf[:, h, :], Sf[:, h, :], v3(heb)[:, h, le:le + 1], sub[:, h, :], ALU.mult, ALU.add,
                              [r_Sf, r_heb, RB[5 + ck]], [r_Sf])
                    k.cp("act", Sb[nxt][:], Sf[:], [r_Sf], [r_Sb[nxt]])
                    cur = nxt
                    yield
                hgrn_state["cur"] = cur
                k.act(sqb[:, 0:N4], ob[:, 0:N4], AF.Square, [RB[4]], [r_sqb])
                k.mm(PS[:, 7 * 512: 7 * 512 + N4], onesb[:, :], sqb[:, 0:N4], True, True, [r_onesb, r_sqb], [RB[7]])
                k.act(hq[:, 0:N4], PS[:, 7 * 512: 7 * 512 + N4], AF.Ln, [RB[7]], [r_hq], scale=1.0 / 128, bias=EPS_AP[:, :])
                k.act(hq[:, 0:N4], hq[:, 0:N4], AF.Exp, [r_hq], [r_hq], scale=-0.5)
                yield
                k.tt("dve", hq[:, 0:N4], ob[:, 0:N4], hq[:, 0:N4], ALU.mult, [RB[4], r_hq], [r_hq])
                k.tt("dve", mixT[:, 0:4, :P], v3(hq), v3(hg), ALU.mult, [r_hq, r_hg], [r_mixT])

            def dsa(g, ti, P, tix, key0, nk, nkb, mixT, r_mixT):
                dp = PS[:, 0:11 * 128].rearrange("p (j t) -> p j t", j=11)
                srcs = [Z_QB + 128 * j for j in range(4)] + [Z_KB, Z_KB + 128] + [Z_QI + 128 * j for j in range(4)] + [Z_KI]
                for j, c0 in enumerate(srcs):
                    k.tr(dp[:, j, :P], z[:P, c0:c0 + 128], identf[:P, :P], [r_zB, r_identf], [RB[j // 4]])
                yield
                qzv = qz[:].rearrange("p (pr e) g t -> p pr e g t", e=2)
                qIzv = qIz[:].rearrange("p (j e) t -> p j e t", e=2)
                for e_ in range(2):
                    ps_ = slice(64 * e_, 64 * e_ + 64)
                    k.cp("act", qzv[ps_, :, e_, :, :P], dp[ps_, 0:4, :P].rearrange("p (pr g) t -> p pr g t", g=2),
                         [RB[0]], [r_qz])
                    k.cp("act", qIzv[ps_, :, e_, :P], dp[ps_, 6:10, :P], [RB[1], RB[2]], [r_qIz])
                k.cp("dve", KT[:, :, key0:key0 + P], dp[:, 4:6, :P], [RB[1]], [r_KT])
                k.cp("dve", kIT[:, key0:key0 + P], dp[:, 10, :P], [RB[2]], [r_kIT])
                k.cp("pool", wI[:P, :], z[:P, Z_WI:Z_WI + 8], [r_zB], [r_wI])
                k.cp("pool", V1[:P, tix, :, 0:64], z[:P, Z_VB:Z_VB + 256].rearrange("p (n d) -> p n d", d=64),
                     [r_zB], [r_V1])
                yield
                cnt = 0
                r_acc = r_accb[0:(nk + 511) // 512]
                for h in range(8):
                    for kb0 in range(0, nk, 512):
                        ncol = min(512, nk - kb0)
                        ra = r_accb[kb0 // 512]
                        pb = cnt % 4
                        cnt += 1
                        k.mm(PS[:P, pb * 512: pb * 512 + ncol], qIz[:, h, :P], kIT[:, kb0:kb0 + ncol], True, True,
                             [r_qIz, r_kIT], [RB[pb]])
                        k.act(PS[:P, pb * 512: pb * 512 + ncol], PS[:P, pb * 512: pb * 512 + ncol], AF.Relu,
                              [RB[pb]], [RB[pb]])
                        wcol = wI[:P, h:h + 1]
                        if h == 0:
                            k.ts("dve", acc[:P, kb0:kb0 + ncol], PS[:P, pb * 512: pb * 512 + ncol], wcol, None,
                                 ALU.mult, None, [RB[pb], r_wI], [ra])
                        else:
                            k.stt(acc[:P, kb0:kb0 + ncol], PS[:P, pb * 512: pb * 512 + ncol], wcol,
                                  acc[:P, kb0:kb0 + ncol], ALU.mult, ALU.add, [RB[pb], r_wI, ra], [ra])
                        yield
                thr = sm[:P, 8:9]
                nreal = nk if g == 0 else 2112
                yield "BISECT"
                if g == 1:
                    k.memset("dve", acc[:P, 2112:NKEYMAX], -1e30, r_acc)
                if nk <= 256:
                    if g == 0:
                        k.memset("dve", acc[0:64, nk - 64:nk], -1e30, r_acc)
                    k.memset("dve", thr, -1e29, [r_sm])
                else:
                    S.op("dve", lambda e: e.tensor_reduce(out=sm[:P, 4:5], in_=acc[:P, 0:nreal], axis=AX.X, op=ALU.max),
                         r_acc, [r_sm])
                    S.op("dve", lambda e: e.tensor_reduce(out=sm[:P, 5:6], in_=acc[:P, 0:nreal], axis=AX.X, op=ALU.min),
                         r_acc, [r_sm])
                    if g == 0:
                        k.memset("dve", acc[0:64, nk - 64:nk], -1e30, r_acc)
                    k.memset("dve", kbc[:P, :], float(nk - 511), [r_kbc])
                    k.tt("dve", sm[:P, 6:7], sm[:P, 4:5], sm[:P, 5:6], ALU.subtract, [r_sm], [r_sm])
                    k.ts("dve", sm[:P, 6:7], sm[:P, 6:7], 1.0001, 1e-6, ALU.mult, ALU.add, [r_sm], [r_sm])
                    k.ts("dve", nhwh[:P, :], pow2[:P, :], sm[:P, 6:7], None, ALU.mult, None, [r_pow2, r_sm], [r_hwtab])
                    k.ts("dve", sm[:P, 11:12], sm[:P, 6:7], -0.5, None, ALU.mult, None, [r_sm], [r_sm])
                    negmid = sm[:P, 7:8]
                    k.stt(negmid, sm[:P, 5:6], -1.0, sm[:P, 11:12], ALU.mult, ALU.add, [r_sm], [r_sm])
                    yield
                    for n in range(NIT):
                        k.act(nmask[:P, 0:nk], acc[:P, 0:nk], AF.Sign, r_acc + [r_sm], [r_nmask, r_sm],
                              bias=negmid, accum=sm[:P, 9:10])
                        k.ts("dve", sm[:P, 10:11], sm[:P, 9:10], float(511.5 - nk), -0.5, ALU.is_ge, ALU.add,
                             [r_sm], [r_sm])
                        k.stt(negmid, sm[:P, 10:11], nhwh[:P, n:n + 1], negmid, ALU.mult, ALU.add, [r_sm, r_hwtab], [r_sm])
                        yield
                    k.stt(thr, negmid, -1.0, nhwh[:P, NIT:NIT + 1], ALU.mult, ALU.add, [r_sm, r_hwtab], [r_sm])
                yield "BISECT_END"
                k.ts("dve", nmask[:P, 0:nk], acc[:P, 0:nk], thr, NEG, ALU.is_lt, ALU.mult, r_acc + [r_sm], [r_nmask])
                yield
                if DEBUG and g == 0 and ti == 2:
                    k.dma("sp", dbg_acc[:, 0:nk], acc[:, 0:nk], r_acc, [])
                for q0 in range(0, nkb, 4):
                    nq = min(4, nkb - q0)
                    pbk = 2 + (q0 // 4) % 2
                    tp = PS[:, pbk * 512: pbk * 512 + 256].bitcast(BF16).rearrange("p (j t) -> p j t", j=4)
                    for j in range(nq):
                        kb = q0 + j
                        k.tr(tp[:, j, :P], nmask[:P, kb * 128:(kb + 1) * 128], identb[:P, :P], [r_nmask, r_identb],
                             [RB[pbk]])
                    k.cp("act", nmT[:, q0:q0 + nq, :P], tp[:, 0:nq, :P], [RB[pbk]], [r_nmT])
                    yield
                oacc = [PS[:, (2 + gg) * 512: (2 + gg) * 512 + 260].rearrange("p (n d) -> p n d", n=4) for gg in range(2)]
                for gg in range(2):
                    k.mm(PS[:P, (2 + gg) * 512:(2 + gg) * 512 + 512], zb512[:, 0:P], zb512[:, 0:512], True, False,
                         [r_zb], [RB[2 + gg]])
                steps = [(n, kb, min(2, nkb - kb)) for n in range(4) for kb in range(0, nkb, 2)]

                def lg_of(i, w):
                    pb = i % 2
                    return PS[:, pb * 512: pb * 512 + w * 2 * P]

                def emit_lg(i):
                    n, kb, w = steps[i]
                    blk = n // 2
                    for j in range(w):
                        lgv = PS[:, (i % 2) * 512 + j * 2 * P: (i % 2) * 512 + (j + 1) * 2 * P].rearrange(
                            "p (g t) -> p g t", g=2)
                        k.mm(lgv, KT[:, blk, (kb + j) * 128:(kb + j + 1) * 128], qz[:, n, :, :P], True, False,
                             [r_KT, r_qz], [RB[i % 2]])
                        k.mm(lgv, identb[:, :], nmT[:, kb + j, :P].unsqueeze(1).broadcast_to([128, 2, P]), False, True,
                             [r_identb, r_nmT], [RB[i % 2]])

                emit_lg(0)
                for i, (n, kb, w) in enumerate(steps):
                    if i + 1 < len(steps):
                        emit_lg(i + 1)
                    slot = i % 2
                    k.act(pT2[slot][:, 0:w * 2 * P], lg_of(i, w), AF.Exp, [RB[i % 2]], [r_pT[slot]], scale=0.125)
                    for j in range(w):
                        for gg in range(2):
                            k.mm(oacc[gg][:P, n, :], pT2[slot][:, j * 2 * P + gg * P: j * 2 * P + (gg + 1) * P],
                                 V1[:, kb + j, n, :], False, False, [r_pT[slot], r_V1], [RB[2 + gg]])
                    yield
                mbv = mixB[:, :].rearrange("p (n g d) -> p n g d", n=4, g=2)
                for gg in range(2):
                    k.recip(sm[:P, 16 + 4 * gg:20 + 4 * gg], oacc[gg][:P, :, 64], [RB[2 + gg]], [r_sm])
                    k.tt("dve", mbv[:P, :, gg, :], oacc[gg][:P, :, 0:64],
                         sm[:P, 16 + 4 * gg:20 + 4 * gg].unsqueeze(2).broadcast_to([P, 4, 64]), ALU.mult,
                         [RB[2 + gg], r_sm], [r_mixB])
                yield
                mp = PS[:, 0:256].bitcast(BF16).rearrange("p (j t) -> p j t", j=4)
                for j in range(4):
                    k.tr(mp[:, j, :P], mixB[:P, j * 128:(j + 1) * 128], identb[:P, :P], [r_mixB, r_identb], [RB[0]])
                k.cp("act", mixT[:, 4:8, :P], mp[:, :, :P], [RB[0]], [r_mixT])

            def sample_cache_prep():
                k.memset("dve", KT[:, :, 2112:NKEYMAX], 0.0, [r_KT])
                k.memset("dve", kIT[:, 2112:NKEYMAX], 0.0, [r_kIT])
                k.cp("pool", V1[64:128, 16, :, 0:64], zb512[64:128, 0:256].rearrange("p (n d) -> p n d", d=64),
                     [r_zb], [r_V1])
                cp_ = PS[:, 2048:2048 + 3 * 128].rearrange("p (j t) -> p j t", j=3)
                for ct in range(16):
                    k.dma("sp", cst[:, 0:256], ck_d[ct * 128:(ct + 1) * 128, :], [], [r_cst])
                    k.dma("sp", cst[:, 256:512], cv_d[ct * 128:(ct + 1) * 128, :], [], [r_cst])
                    k.dma("sp", cst[:, 512:576], cki_d[ct * 128:(ct + 1) * 128, :], [], [r_cst])
                    k.cp("dve", cst[:, 576:640], cst[:, 512:576], [r_cst], [r_cst])
                    for j, c0 in enumerate((0, 128, 512)):
                        k.tr(cp_[:, j, :], cst[:, c0:c0 + 128], identf[:, :], [r_cst, r_identf], [RB[4]])
                    k.cp("act", KT[:, :, ct * 128:(ct + 1) * 128], cp_[:, 0:2, :], [RB[4]], [r_KT])
                    k.cp("dve", kIT[:, ct * 128:(ct + 1) * 128], cp_[:, 2, :], [RB[4]], [r_kIT])
                    k.cp("pool", V1[:, ct, :, 0:64], cst[:, 256:512].rearrange("p (n d) -> p n d", d=64), [r_cst], [r_V1])

            EPS_T = sbt(pa, "eps_t", [128, 1], F32); r_eps = Res()
            EPS_AP = EPS_T
            k.memset("dve", EPS_T[:], EPS, [r_eps])
            hgrn_state = {"cur": 0}

            if CFG["A"]:
                prep_group(0)
                nt_ = CFG["ntiles"]
                drain(zstage(0, 0))
                for ti in range(nt_):
                    last = ti + 1 >= nt_
                    tileA(0, ti, None if last else zstage(0, ti + 1), hgrn_of(0, 0) if ti == 0 else None,
                          None if last else hgrn_of(0, ti + 1))
                k.dma("sp", S_p.rearrange("h k v -> k h v"), Sf[:], [r_Sf], [])
            if CFG["A"] and CFG["sample"]:
                k.dma("sp", Sf[:], s0_d.rearrange("h k v -> k h v"), [], [r_Sf])
                k.cp("act", Sb[hgrn_state["cur"]][:], Sf[:], [r_Sf], [r_Sb[hgrn_state["cur"]]])
                if CFG["scp"]:
                    sample_cache_prep()
                prep_group(1)
                drain(zstage(1, 0))
                tileA(1, 0, None, hgrn_of(1, 0), None)
                k.dma("sp", S_s.rearrange("h k v -> k h v"), Sf[:], [r_Sf], [])
            if DEBUG:
                k.dma("sp", dbg_x1, x1S, r_x1S, [])
            S.flush()

        with ExitStack() as pb_:
            W1 = sbt(pb_, "W1", [128, 8, 4 * D], BF16); r_w1 = [Res() for _ in range(8)]
            W2 = sbt(pb_, "W2", [128, 32, D], BF16); r_w2 = [Res() for _ in range(8)]
            GAM2B = sbt(pb_, "GAM2B", [128, D], F32); r_gam2 = Res()
            SH2B = sbt(pb_, "SH2B", [128, D], F32); r_sh2 = Res()
            G2B = sbt(pb_, "G2B", [128, D], F32); r_g2 = Res()
            NFB = sbt(pb_, "NFB", [128, D], F32); r_nf = Res()
            x1t = [sbt(pb_, "x1t%d" % i, [128, D], F32) for i in range(4)]
            r_x1t = [Res() for _ in range(4)]
            tmpf = sbt(pb_, "tmpf", [128, D], F32); r_tmpf = Res()
            tmpg = sbt(pb_, "tmpg", [128, D], F32); r_tmpg = Res()
            h2 = sbt(pb_, "h2", [128, D], BF16); r_h2 = Res()
            h2T = [sbt(pb_, "h2T%d" % i, [128, 8, 256], BF16) for i in range(2)]
            r_h2T = [Res(), Res()]
            smb2 = sbt(pb_, "smb2", [128, 16], F32); r_smb2 = Res()
            rr = [sbt(pb_, "rr%d" % i, [128, 256], F32) for i in range(2)]
            r_rr = [Res(), Res()]
            uT = sbt(pb_, "uT", [128, 32, 256], BF16); r_uT = Res()
            smb = sbt(pb_, "smb", [128, 16], F32); r_smb = Res()
            epsb = sbt(pb_, "epsb", [128, 1], F32); r_epsb = Res()
            k.memset("dve", epsb[:], EPS, [r_epsb])

            w1v = w_ff1_d.rearrange("(c p) n -> p c n", p=128)
            for j in range(8):
                k.dma("pool", W1[:, :, j * 512:(j + 1) * 512], w1v[:, :, j * 512:(j + 1) * 512], [], [r_w1[j]])
            w2v = w_ff2_d.rearrange("(c p) n -> p c n", p=128)
            for j in range(8):
                k.dma("pool", W2[:, 4 * j:4 * j + 4, :], w2v[:, 4 * j:4 * j + 4, :], [], [r_w2[j]])
            k.dma("sp", NFB[:], normf_d.partition_broadcast(128), [], [r_nf])

            def prepB(g):
                k.dma("sp", SH2B[:], modS[g, 3 * D:4 * D].partition_broadcast(128), [], [r_sh2])
                k.dma("sp", GAM2B[:], modS[g, 4 * D:5 * D].partition_broadcast(128), [], [r_gam2])
                k.dma("sp", G2B[:], modS[g, 5 * D:6 * D].partition_broadcast(128), [], [r_g2])
                k.dma("sp", tmpf[:], norm2_d.partition_broadcast(128), [], [r_tmpf])
                k.ts("dve", GAM2B[:], GAM2B[:], 1.0, None, ALU.add, None, [r_gam2], [r_gam2])
                k.tt("dve", GAM2B[:], GAM2B[:], tmpf[:], ALU.mult, [r_gam2, r_tmpf], [r_gam2])

            def prologueB(g, bi, par):
                TB = 256 if g == 0 else 64
                tok0 = bi * 256 if g == 0 else TP
                ntl = 2 if g == 0 else 1
                P = 128 if g == 0 else 64
                h2Tp = PS[:, 0:1024].bitcast(BF16).rearrange("p (c t) -> p c t", c=8)
                for tl in range(ntl):
                    xb_ = x1t[2 * par + tl]
                    rx = r_x1t[2 * par + tl]
                    k.dma("sp", xb_[:P, :], x1S[tok0 + tl * 128: tok0 + tl * 128 + P, :], [], [rx])
                    k.act(tmpg[:P, :], xb_[:P, :], AF.Square, [rx], [r_tmpg, r_smb], accum=smb[:P, 0:1])
                    k.act(smb[:P, 1:2], smb[:P, 0:1], AF.Ln, [r_smb, r_epsb], [r_smb], scale=1.0 / D, bias=epsb[:P, :])
                    k.act(smb[:P, 2:3], smb[:P, 1:2], AF.Exp, [r_smb], [r_smb], scale=-0.5)
                    k.stt(tmpg[:P, :], xb_[:P, :], smb[:P, 2:3], GAM2B[:P, :], ALU.mult, ALU.mult,
                          [rx, r_smb, r_gam2], [r_tmpg])
                    k.tt("pool", h2[:P, :], tmpg[:P, :], SH2B[:P, :], ALU.add, [r_tmpg, r_sh2], [r_h2])
                    for c in range(8):
                        k.tr(h2Tp[:, c, tl * 128: tl * 128 + P], h2[:P, c * 128:(c + 1) * 128], identb[:P, :P],
                             [r_h2, r_identb], [RB[0] if c < 4 else RB[1]])
                k.cp("act", h2T[par][:, 0:4, 0:TB], h2Tp[:, 0:4, 0:TB], [RB[0]], [r_h2T[par]])
                k.cp("dve", h2T[par][:, 4:8, 0:TB], h2Tp[:, 4:8, 0:TB], [RB[1]], [r_h2T[par]])

            def ffn1B(g, bi, par):
                TB = 256 if g == 0 else 64
                for j in range(32):
                    pb = 2 + j % 2
                    for c in range(8):
                        k.mm(PS[:, pb * 512: pb * 512 + TB], W1[:, c, j * 128:(j + 1) * 128], h2T[par][:, c, 0:TB],
                             c == 0, c == 7, [r_w1[j // 4], r_h2T[par]], [RB[pb]])
                    k.act(rr[j % 2][:, 0:TB], PS[:, pb * 512: pb * 512 + TB], AF.Relu, [RB[pb]], [r_rr[j % 2]])
                    k.tt("pool", uT[:, j, 0:TB], rr[j % 2][:, 0:TB], rr[j % 2][:, 0:TB], ALU.mult, [r_rr[j % 2]], [r_uT])

            def ffn2B(g, bi, par):
                tok0 = bi * 256 if g == 0 else TP
                ysrc = y_p if g == 0 else y_s
                ntl = 2 if g == 0 else 1
                P = 128 if g == 0 else 64
                for tl in range(ntl):
                    xb_ = x1t[2 * par + tl]
                    rx = r_x1t[2 * par + tl]
                    for hh in range(2):
                        pb = 4 + (2 * tl + hh) % 4
                        for j in range(32):
                            k.mm(PS[:P, pb * 512:(pb + 1) * 512], uT[:, j, tl * 128: tl * 128 + P],
                                 W2[:, j, hh * 512:(hh + 1) * 512], j == 0, j == 31, [r_uT, r_w2[j // 4]], [RB[pb]])
                        cs = slice(hh * 512, (hh + 1) * 512)
                        k.tt("dve", tmpf[:P, cs], PS[:P, pb * 512:(pb + 1) * 512], G2B[:P, cs], ALU.mult,
                             [RB[pb], r_g2], [r_tmpf])
                        k.tt("pool", xb_[:P, cs], tmpf[:P, cs], xb_[:P, cs], ALU.add, [r_tmpf, rx], [rx])
                    k.act(tmpf[:P, :], xb_[:P, :], AF.Square, [rx], [r_tmpf, r_smb2], accum=smb2[:P, 4:5])
                    k.act(smb2[:P, 5:6], smb2[:P, 4:5], AF.Ln, [r_smb2, r_epsb], [r_smb2], scale=1.0 / D, bias=epsb[:P, :])
                    k.act(smb2[:P, 6:7], smb2[:P, 5:6], AF.Exp, [r_smb2], [r_smb2], scale=-0.5)
                    k.stt(tmpf[:P, :], xb_[:P, :], smb2[:P, 6:7], NFB[:P, :], ALU.mult, ALU.mult,
                          [rx, r_smb2, r_nf], [r_tmpf])
                    yo = tok0 - (0 if g == 0 else TP) + tl * 128
                    k.dma("sp", ysrc[yo:yo + P, :], tmpf[:P, :], [r_tmpf], [])

            if CFG["B"]:
                prepB(0)
                nb = CFG["nblk"]
                prologueB(0, 0, 0)
                for bi in range(nb):
                    par = bi % 2
                    ffn1B(0, bi, par)
                    if bi + 1 < nb:
                        prologueB(0, bi + 1, 1 - par)
                    ffn2B(0, bi, par)
                if CFG["sample"]:
                    prepB(1)
                    prologueB(1, 0, 0)
                    ffn1B(1, 0, 0)
                    ffn2B(1, 0, 0)
            S.flush()
    return nc


def _consts():
    ident = np.eye(128, dtype=np.float32)
    s = np.arange(128)[:, None]
    t = np.arange(128)[None, :]
    tri = ((s <= t) & (s // 64 == t // 64)).astype(np.float32)
    reset = np.ones((128, 512), np.float32)
    reset[:, 0::64] = 0.0
    p2 = -(2.0 ** -(np.arange(NIT + 1) + 1.0))
    pow2 = np.tile(p2[None, :], (128, 1)).astype(np.float32)
    half = 8
    inv = np.power(np.float32(500000.0), -np.arange(half, dtype=np.float32) * np.float32(2.0 / 16)).astype(np.float32)
    pos = np.zeros((17, 128), np.float32)
    for i in range(16):
        pos[i] = np.arange(128) + 128 * i
    pos[16, :64] = 2048 + np.arange(64)
    ang = pos[:, :, None].astype(np.float32) * inv[None, None, :]
    cos = np.cos(ang).astype(np.float32).transpose(1, 0, 2).copy()
    sin = np.sin(ang).astype(np.float32).transpose(1, 0, 2).copy()
    return {"c_ident": ident, "c_tri": tri, "c_reset": reset, "c_pow2": pow2, "c_cos": cos, "c_sin": sin}


_NC_CACHE = {}


def kernel(x_prompt, x_sample, cache_k, cache_v, cache_k_idx, state_hgrn, c_prompt, c_sample,
           w_mod, b_mod, norm1, w_in, lb_logits, g_norm_a, w_out, norm2, w_ff1, w_ff2, norm_f, _trace=False):
    f = lambda a: np.ascontiguousarray(np.asarray(a, dtype=np.float32))
    if "nc" not in _NC_CACHE:
        _NC_CACHE["nc"] = build_program()
    nc = _NC_CACHE["nc"]
    consts = _consts()
    shared = {"w_mod": f(w_mod)[0], "b_mod": f(b_mod)[0], "norm1": f(norm1)[0], "w_in": f(w_in)[0],
              "lb_logits": f(lb_logits), "g_norm_a": f(g_norm_a)[0], "w_out": f(w_out)[0], "norm2": f(norm2)[0],
              "w_ff1": f(w_ff1)[0], "w_ff2": f(w_ff2)[0], "norm_f": f(norm_f)}
    shared.update(consts)
    xp, xs = f(x_prompt), f(x_sample)
    ck, cv, cki, s0 = f(cache_k)[0], f(cache_v)[0], f(cache_k_idx)[0], f(state_hgrn)[0]
    cp_, cs_ = f(c_prompt), f(c_sample)
    in_maps = []
    for b in range(8):
        m = dict(shared)
        m["x_p"] = xp[b]
        m["x_s"] = xs[b]
        m["ck"] = ck[b].reshape(PAST, 256)
        m["cv"] = cv[b].reshape(PAST, 256)
        m["cki"] = cki[b]
        m["s0"] = s0[b]
        m["c2"] = np.stack([cp_[b], cs_[b]], 0)
        in_maps.append(m)
    res = run_bass_kernel_spmd(nc, in_maps, core_ids=list(range(8)), **({"trace": True} if _trace else {}))
    R = res.results
    st = lambda name: np.stack([np.asarray(R[b][name], dtype=np.float32) for b in range(8)], 0)
    outs = (st("y_p"), st("y_s"),
            st("k_p").reshape(1, 8, TP, 4, 64), st("v_p").reshape(1, 8, TP, 4, 64), st("ki_p").reshape(1, 8, TP, 64),
            st("S_p").reshape(1, 8, 4, 128, 128),
            st("k_s").reshape(1, 8, TS, 4, 64), st("v_s").reshape(1, 8, TS, 4, 64), st("ki_s").reshape(1, 8, TS, 64),
            st("S_s").reshape(1, 8, 4, 128, 128))
    if DEBUG:
        _NC_CACHE["dbg"] = {n: np.stack([np.asarray(R[b][n]) for b in range(8)], 0)
                            for n in ("dbg_x1", "dbg_mod", "dbg_z", "dbg_mix", "dbg_acc")}
        _NC_CACHE["res"] = res
    return outs
```
